# Optimizing a Trainium2 kernel written in Bass

```python
import math
import jax, jax.numpy as jnp
from jax import lax
import numpy as np

D_MODEL = 2048
BATCH = 16
SEQ = 2048
DEPTH = 2

MEM_LEN = 256
N_EVEN = (DEPTH + 1) // 2
N_ODD = DEPTH // 2
D_CONV = D_MODEL // 2
CONV_WIDTH = 31
D_DIFF = D_MODEL // 2
DIFF_DH = 64
DIFF_DV = 2 * DIFF_DH
DIFF_HEADS = D_DIFF // DIFF_DV
Q_BLOCK = 128
S5_GROUP_CH = 16
S5_GROUPS = D_MODEL // S5_GROUP_CH
S5_STATE = 64
S5_DT_MIN = 1e-3
S5_DT_MAX = 1e-1
XA_HEADS = 4
XA_DH = D_MODEL // XA_HEADS
D_FF = -(-8 * D_MODEL // (3 * 256)) * 256
FFN_CONV_WIDTH = 3
RMS_EPS = 1e-6
LN_EPS = 1e-5

kernel_name = "hybrid_conv_diffattn_s5_encoder"


def rms_norm(x, g, eps=RMS_EPS):
    xf = x.astype(jnp.float32)
    y = xf * lax.rsqrt(jnp.mean(xf * xf, axis=-1, keepdims=True) + eps)
    return (y * g.astype(jnp.float32)).astype(x.dtype)


def layer_norm(x, g, b, eps=LN_EPS):
    xf = x.astype(jnp.float32)
    mu = jnp.mean(xf, axis=-1, keepdims=True)
    var = jnp.mean(jnp.square(xf - mu), axis=-1, keepdims=True)
    y = (xf - mu) * lax.rsqrt(var + eps) * g.astype(jnp.float32) + b.astype(jnp.float32)
    return y.astype(x.dtype)


def depthwise_conv(x, w, b):
    y = lax.conv_general_dilated(
        x, w[:, None, :].astype(x.dtype), window_strides=(1,), padding="SAME",
        dimension_numbers=("NWC", "WIO", "NWC"), feature_group_count=x.shape[-1])
    return y + b.astype(x.dtype)


def alibi_slopes(n_heads):
    return jnp.exp2(-8.0 * jnp.arange(1, n_heads + 1, dtype=jnp.float32) / n_heads)


def conformer_conv(val, gate, dw, db, ln_g, ln_b):
    h = val * jax.nn.sigmoid(gate)
    h = depthwise_conv(h, dw, db)
    h = layer_norm(h, ln_g, ln_b)
    return jax.nn.silu(h)


def diff_attention(q, k, v, lq1, lk1, lq2, lk2, subln_g, lambda_init):
    bsz, s_len = q.shape[0], q.shape[1]
    lam = (jnp.exp(jnp.sum(lq1.astype(jnp.float32) * lk1.astype(jnp.float32)))
           - jnp.exp(jnp.sum(lq2.astype(jnp.float32) * lk2.astype(jnp.float32))) + lambda_init)
    slopes = alibi_slopes(DIFF_HEADS)
    pos = jnp.arange(s_len, dtype=jnp.float32)
    n_blk = s_len // Q_BLOCK
    qb = q.reshape(bsz, n_blk, Q_BLOCK, DIFF_HEADS, 2, DIFF_DH).transpose(1, 0, 2, 3, 4, 5)
    pb = pos.reshape(n_blk, Q_BLOCK)
    scale = DIFF_DH ** -0.5

    def one_block(args):
        qi, pi = args
        s = jnp.einsum("bqhcd,bkhcd->bchqk", qi, k).astype(jnp.float32) * scale
        bias = -slopes[:, None, None] * jnp.abs(pi[:, None] - pos[None, :])
        p = jax.nn.softmax(s + bias[None, None], axis=-1)
        w = (p[:, 0] - lam * p[:, 1]).astype(v.dtype)
        return jnp.einsum("bhqk,bkhe->bqhe", w, v)

    o = lax.map(one_block, (qb, pb))
    o = o.transpose(1, 0, 2, 3, 4).reshape(bsz, s_len, DIFF_HEADS, DIFF_DV)
    o = rms_norm(o, subln_g, LN_EPS) * (1.0 - lambda_init)
    return o.reshape(bsz, s_len, DIFF_HEADS * DIFF_DV)


def mixer_conv_diff(xn, w_in, conv_dw, conv_db, conv_ln_g, conv_ln_b,
                    lq1, lk1, lq2, lk2, subln_g, w_out, lambda_init):
    bsz, s_len, _ = xn.shape
    z = xn @ w_in
    a_val, a_gate, q, k, v = jnp.split(
        z, [D_CONV, 2 * D_CONV, 2 * D_CONV + D_DIFF, 2 * D_CONV + 2 * D_DIFF], axis=-1)
    a_out = conformer_conv(a_val, a_gate, conv_dw, conv_db, conv_ln_g, conv_ln_b)
    b_out = diff_attention(
        q.reshape(bsz, s_len, DIFF_HEADS, 2, DIFF_DH),
        k.reshape(bsz, s_len, DIFF_HEADS, 2, DIFF_DH),
        v.reshape(bsz, s_len, DIFF_HEADS, DIFF_DV),
        lq1, lk1, lq2, lk2, subln_g, lambda_init)
    return jnp.concatenate([a_out, b_out], axis=-1) @ w_out


def s5_discretize(lam_re, lam_im, log_dt, b_re, b_im):
    dt = jnp.exp(log_dt.astype(jnp.float32))[:, None]
    lr = lam_re.astype(jnp.float32)
    li = lam_im.astype(jnp.float32)
    mag = jnp.exp(lr * dt)
    lb_re = mag * jnp.cos(li * dt)
    lb_im = mag * jnp.sin(li * dt)
    den = lr * lr + li * li
    f_re = ((lb_re - 1.0) * lr + lb_im * li) / den
    f_im = (lb_im * lr - (lb_re - 1.0) * li) / den
    br = b_re.astype(jnp.float32)
    bi = b_im.astype(jnp.float32)
    bb_re = f_re[..., None] * br - f_im[..., None] * bi
    bb_im = f_re[..., None] * bi + f_im[..., None] * br
    return lb_re, lb_im, bb_re, bb_im


def _linear_recurrence_combine(e1, e2):
    a1r, a1i, b1r, b1i = e1
    a2r, a2i, b2r, b2i = e2
    return (a2r * a1r - a2i * a1i,
            a2r * a1i + a2i * a1r,
            a2r * b1r - a2i * b1i + b2r,
            a2r * b1i + a2i * b1r + b2i)


def mixer_s5(xn, lam_re, lam_im, log_dt, b_re, b_im, c_re, c_im, d_skip, w_val, w_gate):
    bsz, s_len, _ = xn.shape
    u = xn.reshape(bsz, s_len, S5_GROUPS, S5_GROUP_CH).astype(jnp.float32)
    disc = [s5_discretize(lam_re[dr], lam_im[dr], log_dt[dr], b_re[dr], b_im[dr]) for dr in (0, 1)]
    cr = c_re.astype(jnp.float32)
    ci = c_im.astype(jnp.float32)

    def one_sequence(us):
        outs = []
        for dr, rev in ((0, False), (1, True)):
            lb_re, lb_im, bb_re, bb_im = disc[dr]
            bu_re = jnp.einsum("sgc,gpc->sgp", us, bb_re)
            bu_im = jnp.einsum("sgc,gpc->sgp", us, bb_im)
            a_re = jnp.broadcast_to(lb_re, bu_re.shape)
            a_im = jnp.broadcast_to(lb_im, bu_re.shape)
            _, _, h_re, h_im = lax.associative_scan(
                _linear_recurrence_combine, (a_re, a_im, bu_re, bu_im), reverse=rev, axis=0)
            outs.append(jnp.einsum("sgp,gcp->sgc", h_re, cr[dr])
                        - jnp.einsum("sgp,gcp->sgc", h_im, ci[dr]))
        return outs[0] + outs[1]

    y = lax.map(one_sequence, u).reshape(bsz, s_len, D_MODEL).astype(xn.dtype)
    y = y + d_skip * xn
    g = jax.nn.gelu(y)
    return (g @ w_val) * jax.nn.sigmoid(g @ w_gate)


def memory_cross_attention(xn, memn, wq, wk, wv, wo):
    bsz, s_len, _ = xn.shape
    m_len = memn.shape[1]
    q = (xn @ wq).reshape(bsz, s_len, XA_HEADS, XA_DH)
    k = (memn @ wk).reshape(bsz, m_len, XA_HEADS, XA_DH)
    v = (memn @ wv).reshape(bsz, m_len, XA_HEADS, XA_DH)
    s = jnp.einsum("bqhd,bkhd->bhqk", q, k).astype(jnp.float32) * (XA_DH ** -0.5)
    p = jax.nn.softmax(s, axis=-1).astype(v.dtype)
    o = jnp.einsum("bhqk,bkhd->bqhd", p, v).reshape(bsz, s_len, D_MODEL)
    return o @ wo


def conv_ffn(xn, w_up, dw, db, w_down):
    h = depthwise_conv(xn @ w_up, dw, db)
    g, v = jnp.split(h, 2, axis=-1)
    return (jax.nn.silu(g) * v) @ w_down


def setup_inputs(seed: int = 0) -> dict:
    key = jax.random.key(seed)
    ks = list(jax.random.split(key, 48))

    def nrm(shape, scale):
        return scale * jax.random.normal(ks.pop(), shape, jnp.float32)

    def gain(shape):
        return 1.0 + nrm(shape, 0.05)

    n_in = 2 * D_CONV + 3 * D_DIFF
    lam_im_base = jnp.pi * jnp.arange(S5_STATE, dtype=jnp.float32)
    log_dt = jax.random.uniform(ks.pop(), (N_ODD, 2, S5_GROUPS), jnp.float32,
                                math.log(S5_DT_MIN), math.log(S5_DT_MAX))
    return {
        "x": nrm((BATCH, SEQ, D_MODEL), 1.0),
        "mem": nrm((BATCH, MEM_LEN, D_MODEL), 1.0),
        "norm_mix_g": gain((DEPTH, D_MODEL)),
        "ab_w_in": nrm((N_EVEN, D_MODEL, n_in), D_MODEL ** -0.5),
        "conv_dw": nrm((N_EVEN, CONV_WIDTH, D_CONV), CONV_WIDTH ** -0.5),
        "conv_db": nrm((N_EVEN, D_CONV), 0.02),
        "conv_ln_g": gain((N_EVEN, D_CONV)),
        "conv_ln_b": nrm((N_EVEN, D_CONV), 0.02),
        "diff_lq1": nrm((N_EVEN, DIFF_DH), 0.1),
        "diff_lk1": nrm((N_EVEN, DIFF_DH), 0.1),
        "diff_lq2": nrm((N_EVEN, DIFF_DH), 0.1),
        "diff_lk2": nrm((N_EVEN, DIFF_DH), 0.1),
        "diff_subln_g": gain((N_EVEN, DIFF_DV)),
        "ab_w_out": nrm((N_EVEN, D_CONV + D_DIFF, D_MODEL), (D_CONV + D_DIFF) ** -0.5),
        "s5_lam_re": -0.5 + nrm((N_ODD, 2, S5_GROUPS, S5_STATE), 0.01),
        "s5_lam_im": lam_im_base + nrm((N_ODD, 2, S5_GROUPS, S5_STATE), 0.01),
        "s5_log_dt": log_dt,
        "s5_b_re": nrm((N_ODD, 2, S5_GROUPS, S5_STATE, S5_GROUP_CH), (2 * S5_GROUP_CH) ** -0.5),
        "s5_b_im": nrm((N_ODD, 2, S5_GROUPS, S5_STATE, S5_GROUP_CH), (2 * S5_GROUP_CH) ** -0.5),
        "s5_c_re": nrm((N_ODD, 2, S5_GROUPS, S5_GROUP_CH, S5_STATE), (2 * S5_STATE) ** -0.5),
        "s5_c_im": nrm((N_ODD, 2, S5_GROUPS, S5_GROUP_CH, S5_STATE), (2 * S5_STATE) ** -0.5),
        "s5_d": nrm((N_ODD, D_MODEL), 1.0),
        "s5_w_val": nrm((N_ODD, D_MODEL, D_MODEL), D_MODEL ** -0.5),
        "s5_w_gate": nrm((N_ODD, D_MODEL, D_MODEL), D_MODEL ** -0.5),
        "norm_xa_g": gain((DEPTH, D_MODEL)),
        "norm_mem_g": gain((DEPTH, D_MODEL)),
        "xa_wq": nrm((DEPTH, D_MODEL, D_MODEL), D_MODEL ** -0.5),
        "xa_wk": nrm((DEPTH, D_MODEL, D_MODEL), D_MODEL ** -0.5),
        "xa_wv": nrm((DEPTH, D_MODEL, D_MODEL), D_MODEL ** -0.5),
        "xa_wo": nrm((DEPTH, D_MODEL, D_MODEL), D_MODEL ** -0.5),
        "norm_ffn_g": gain((DEPTH, D_MODEL)),
        "ffn_w_up": nrm((DEPTH, D_MODEL, 2 * D_FF), D_MODEL ** -0.5),
        "ffn_dw": nrm((DEPTH, FFN_CONV_WIDTH, 2 * D_FF), FFN_CONV_WIDTH ** -0.5),
        "ffn_db": nrm((DEPTH, 2 * D_FF), 0.02),
        "ffn_w_down": nrm((DEPTH, D_FF, D_MODEL), D_FF ** -0.5),
        "final_g": gain((D_MODEL,)),
    }


def reference(x, mem, norm_mix_g, ab_w_in, conv_dw, conv_db, conv_ln_g, conv_ln_b,
              diff_lq1, diff_lk1, diff_lq2, diff_lk2, diff_subln_g, ab_w_out,
              s5_lam_re, s5_lam_im, s5_log_dt, s5_b_re, s5_b_im, s5_c_re, s5_c_im, s5_d,
              s5_w_val, s5_w_gate, norm_xa_g, norm_mem_g, xa_wq, xa_wk, xa_wv, xa_wo,
              norm_ffn_g, ffn_w_up, ffn_dw, ffn_db, ffn_w_down, final_g):
    for layer in range(DEPTH):
        h = rms_norm(x, norm_mix_g[layer])
        i = layer // 2
        if layer % 2 == 0:
            lambda_init = 0.8 - 0.6 * math.exp(-0.3 * layer)
            x = x + mixer_conv_diff(h, ab_w_in[i], conv_dw[i], conv_db[i], conv_ln_g[i], conv_ln_b[i],
                                    diff_lq1[i], diff_lk1[i], diff_lq2[i], diff_lk2[i],
                                    diff_subln_g[i], ab_w_out[i], lambda_init)
        else:
            x = x + mixer_s5(h, s5_lam_re[i], s5_lam_im[i], s5_log_dt[i], s5_b_re[i], s5_b_im[i],
                             s5_c_re[i], s5_c_im[i], s5_d[i], s5_w_val[i], s5_w_gate[i])
        x = x + memory_cross_attention(rms_norm(x, norm_xa_g[layer]), rms_norm(mem, norm_mem_g[layer]),
                                       xa_wq[layer], xa_wk[layer], xa_wv[layer], xa_wo[layer])
        x = x + conv_ffn(rms_norm(x, norm_ffn_g[layer]), ffn_w_up[layer], ffn_dw[layer],
                         ffn_db[layer], ffn_w_down[layer])
    return rms_norm(x, final_g)
```

```python
import math
import os
from contextlib import ExitStack

import numpy as np
import ml_dtypes

import concourse.bass as bass
import concourse.mybir as mybir
from concourse.bass_utils import run_bass_kernel_spmd

F32 = mybir.dt.float32
BF16 = mybir.dt.bfloat16
U8 = mybir.dt.uint8
AF = mybir.ActivationFunctionType
ALU = mybir.AluOpType
AX = mybir.AxisListType

D = 2048
S = 2048
NSEQ = 2
TOK = NSEQ * S
T = 512
NT = TOK // T
DFF = 5632
NFF = DFF // 128
MEM = 256
RMS_EPS = 1e-6
LN_EPS = 1e-5
NCORES = 8

SEM_LIMIT = 50000


class Buf:
    __slots__ = ("name", "w", "r")

    def __init__(self, name):
        self.name = name
        self.w = None
        self.r = []


class Sched:
    ENGS = ["pe", "act", "dve", "pool", "sp"]

    def __init__(self):
        self.ops = []
        self.last = {e: None for e in self.ENGS}
        self.dlast = {}

    def op(self, eng, fn, reads=(), writes=(), dkey=None, extra=()):
        i = len(self.ops)
        deps = set(extra)
        for b in reads:
            if b.w is not None:
                deps.add(b.w)
        for b in writes:
            if b.w is not None:
                deps.add(b.w)
            deps.update(b.r)
        for b in reads:
            b.r.append(i)
        for b in writes:
            b.w = i
            b.r = []
        if eng == "pe":
            deps = set(d for d in deps if self.ops[d][0] != "pe" or self.ops[d][3] is not None)
        self.ops.append([eng, fn, deps, dkey])
        self.last[eng] = i
        if dkey is not None:
            self.dlast[dkey] = i
        return i

    def barrier(self, skip=()):
        deps = [v for v in self.last.values() if v is not None] + [v for k, v in self.dlast.items()
                                                                    if not any(k.startswith(p) for p in skip)]
        for e in self.ENGS:
            self.op(e, None, extra=deps)

    def emit(self, nc, es):
        ops = self.ops
        needed = set()
        for o in ops:
            needed |= o[2]
        ecount = {e: 0 for e in self.ENGS}
        dcount = {}
        ev = {}
        semnames = set()
        for i, (eng, fn, deps, dkey) in enumerate(ops):
            if dkey is not None:
                dcount[dkey] = dcount.get(dkey, 0) + 16
                ev[i] = ("d_" + dkey, dcount[dkey], 16)
                semnames.add("d_" + dkey)
            elif i in needed and fn is not None:
                k = ecount[eng]
                ecount[eng] += 1
                nm = "e_%s_%d" % (eng, k // SEM_LIMIT)
                ev[i] = (nm, k % SEM_LIMIT + 1, 1)
                semnames.add(nm)
        sems = {}
        for nm in sorted(semnames):
            sems[nm] = es.enter_context(nc.semaphore(nm))
        self.nsems = len(sems)
        per_eng = {e: [] for e in self.ENGS}
        for i, o in enumerate(ops):
            per_eng[o[0]].append(i)

        def run_engine(eng_name, e):
            waited = {}
            for i in per_eng[eng_name]:
                _, fn, deps, dkey = ops[i]
                need = {}
                for d in deps:
                    if d not in ev:
                        continue
                    nm, val, _ = ev[d]
                    if need.get(nm, 0) < val:
                        need[nm] = val
                for nm, val in need.items():
                    if waited.get(nm, 0) < val:
                        e.wait_ge(sems[nm], val)
                        waited[nm] = val
                if fn is None:
                    continue
                inst = fn(e)
                if i in ev:
                    nm, val, inc = ev[i]
                    inst.then_inc(sems[nm], inc)

        block = es.enter_context(nc.Block())

        @block.tensor
        def _(e):
            run_engine("pe", e)

        @block.scalar
        def _(e):
            run_engine("act", e)

        @block.vector
        def _(e):
            run_engine("dve", e)

        @block.gpsimd
        def _(e):
            run_engine("pool", e)

        @block.sync
        def _(e):
            run_engine("sp", e)


def _cols(v):
    v = np.asarray(v, np.float32)
    return np.ascontiguousarray(v.reshape(-1, 128).T)


class PBlob:
    def __init__(self):
        self.parts = []
        self.off = {}
        self.n = 0

    def add(self, name, arr):
        arr = np.ascontiguousarray(np.asarray(arr, np.float32))
        assert arr.shape[0] == 128, (name, arr.shape)
        arr = arr.reshape(128, -1)
        self.off[name] = (self.n, arr.shape[1])
        self.parts.append(arr)
        self.n += arr.shape[1]

    def build(self):
        return np.ascontiguousarray(np.concatenate(self.parts, axis=1))


def make_consts():
    c = PBlob()
    c.add("ident", np.eye(128, dtype=np.float32))
    c.add("ones", np.ones((128, 128), np.float32))
    jj = np.arange(3968, dtype=np.float32)[None, :]
    pp = np.arange(128, dtype=np.float32)[:, None]
    c.add("tdist", np.abs(jj - pp - 1920.0))
    k = np.arange(128)
    c.add("sgn_tb", np.where(k < 64, -1.0, 1.0).astype(np.float32).reshape(128, 1))
    c.add("nsgn", np.where(k < 64, 1.0, -1.0).astype(np.float32).reshape(128, 1))
    c.add("gmask", (k[:, None] // 16 == np.arange(8)[None, :]).astype(np.float32))
    sh = np.zeros((128, 128), np.float32)
    sh[k, (k + 64) % 128] = 1.0
    c.add("shift64", sh)
    return c


def make_s5par(inp):
    lam_re = np.asarray(inp["s5_lam_re"][0], np.float32)
    lam_im = np.asarray(inp["s5_lam_im"][0], np.float32)
    log_dt = np.asarray(inp["s5_log_dt"][0], np.float32)
    b_re = np.asarray(inp["s5_b_re"][0], np.float32)
    b_im = np.asarray(inp["s5_b_im"][0], np.float32)
    c_re = np.asarray(inp["s5_c_re"][0], np.float32)
    c_im = np.asarray(inp["s5_c_im"][0], np.float32)
    parts = []
    def two(a):
        t = a.transpose(2, 0, 1).reshape(64, 256)
        return np.concatenate([t, t], axis=0)
    parts.append(two(lam_re))
    parts.append(two(lam_im))
    parts.append(np.broadcast_to(log_dt.reshape(1, 256), (128, 256)))
    for ch in range(16):
        for d in range(2):
            br = b_re[d, 8 * ch:8 * ch + 8].transpose(1, 0, 2).reshape(64, 128)
            bi = b_im[d, 8 * ch:8 * ch + 8].transpose(1, 0, 2).reshape(64, 128)
            cr = c_re[d, 8 * ch:8 * ch + 8].transpose(2, 0, 1).reshape(64, 128)
            ci = c_im[d, 8 * ch:8 * ch + 8].transpose(2, 0, 1).reshape(64, 128)
            parts.append(np.concatenate([br, bi], axis=0))
            parts.append(np.concatenate([bi, br], axis=0))
            parts.append(np.concatenate([cr, ci], axis=0))
    return np.ascontiguousarray(np.concatenate(parts, axis=1).astype(np.float32))


S5PAR_COLS = 768 + 16 * 2 * 3 * 128


def make_params(inp, c):
    for l in range(2):
        c.add("g_mix%d" % l, _cols(inp["norm_mix_g"][l]))
        c.add("g_xa%d" % l, _cols(inp["norm_xa_g"][l]))
        c.add("g_mem%d" % l, _cols(inp["norm_mem_g"][l]))
        c.add("g_ffn%d" % l, _cols(inp["norm_ffn_g"][l]))
        dw = np.asarray(inp["ffn_dw"][l], np.float32)
        c.add("ffn_dw%d" % l, dw.reshape(3, 88, 128).transpose(2, 1, 0))
        c.add("ffn_db%d" % l, _cols(inp["ffn_db"][l]))
    c.add("g_final", _cols(inp["final_g"]))
    cw = np.asarray(inp["conv_dw"][0], np.float32)
    c.add("conv_dw", cw.reshape(31, 8, 128).transpose(2, 1, 0))
    c.add("conv_db", _cols(inp["conv_db"][0]))
    c.add("conv_ln_g", _cols(inp["conv_ln_g"][0]))
    c.add("conv_ln_b", _cols(inp["conv_ln_b"][0]))
    for nm in ("diff_lq1", "diff_lk1", "diff_lq2", "diff_lk2"):
        c.add(nm, np.broadcast_to(np.asarray(inp[nm][0], np.float32)[None, :], (128, 64)))
    c.add("subln_g", np.asarray(inp["diff_subln_g"][0], np.float32).reshape(128, 1))
    c.add("s5_d", _cols(inp["s5_d"][0]))
    return c


WEIGHTS = [
    ("ab_w_in", (2048, 5120)),
    ("xa_wk0", (2048, 2048)), ("xa_wv0", (2048, 2048)),
    ("ab_w_out", (2048, 2048)),
    ("xa_wq0", (2048, 2048)), ("xa_wo0", (2048, 2048)),
    ("ffn_w_up0", (2048, 11264)), ("ffn_w_down0", (5632, 2048)),
    ("xa_wk1", (2048, 2048)), ("xa_wv1", (2048, 2048)),
    ("s5_w_val", (2048, 2048)),
    ("s5_w_gate", (2048, 2048)),
    ("xa_wq1", (2048, 2048)), ("xa_wo1", (2048, 2048)),
    ("ffn_w_up1", (2048, 11264)), ("ffn_w_down1", (5632, 2048)),
]


def build_program(poff, np_cols, phases, debug=()):
    nc = bass.Bass("TRN2", target_bir_lowering=False)
    es = ExitStack()
    sch = Sched()

    def dram(name, shape, dt, kind="Internal"):
        if name in debug:
            kind = "ExternalOutput"
        return nc.dram_tensor(name, list(shape), dt, kind=kind).ap()

    x_in = dram("x_in", [TOK, D], F32, "ExternalInput")
    mem_in = dram("mem_in", [NSEQ * MEM, D], F32, "ExternalInput")
    par_in = dram("par_in", [128, np_cols], F32, "ExternalInput")
    s5par = dram("s5par", [128, S5PAR_COLS], F32, "ExternalInput")
    out = dram("out", [TOK, D], F32, "ExternalOutput")
    w_in = {}
    w_bf = {}
    wbuf = {}
    for nm, (k, n) in WEIGHTS:
        w_in[nm] = dram(nm, [k, n], F32, "ExternalInput")
        w_bf[nm] = dram(nm + "_bf", [k, n], BF16)
        wbuf[nm] = Buf("W_" + nm)
    xT = dram("xT", [D, TOK], F32)
    gluT = dram("gluT", [1024, TOK], BF16)
    qT = dram("qT", [1024, TOK], BF16)
    kT = dram("kT", [1024, TOK], BF16)
    vtok = dram("vtok", [TOK, 1024], BF16)
    catT = dram("catT", [2048, TOK], BF16)
    xnT = dram("xnT", [2048, TOK], BF16)
    b_xT = [Buf("xT%d" % n) for n in range(NT + 1)]
    b_gluT = [Buf("gluT%d" % n) for n in range(NT)]
    b_qT = [Buf("qT%d" % n) for n in range(NT)]
    b_kT = [Buf("kT%d" % n) for n in range(NT)]
    b_v = [Buf("v%d" % n) for n in range(NT)]
    b_catA = [Buf("catA%d" % n) for n in range(NT)]
    b_catB = [Buf("catB%d" % n) for n in range(NSEQ)]
    b_xnT = [Buf("xnT%d" % n) for n in range(NT + 1)]

    ARENA = 196608
    arena = es.enter_context(nc.sbuf_tensor("arena", [128, ARENA], U8))
    psum = [es.enter_context(nc.psum_tensor("ps%d" % i, [128, 512], F32)) for i in range(8)]
    psb = [Buf("ps%d" % i) for i in range(8)]
    ps_i = [0]

    def nextps():
        i = ps_i[0]
        ps_i[0] = (i + 1) % 8
        return psum[i], psb[i]

    class Arena:
        def __init__(self, start=0):
            self.off = start

        def alloc(self, shape, dt, name=None):
            nelem = int(np.prod(shape[1:]))
            nbytes = nelem * (2 if dt == BF16 else 4)
            nbytes_al = (nbytes + 63) // 64 * 64
            assert self.off + nbytes_al <= ARENA, ("arena overflow", name, self.off, nbytes_al)
            ap = arena[:, self.off:self.off + nbytes].bitcast(dt)
            self.off += nbytes_al
            if len(shape) == 3:
                ap = ap.rearrange("p (a b) -> p a b", a=shape[1])
            elif len(shape) == 4:
                ap = ap.rearrange("p (a b c) -> p a b c", a=shape[1], b=shape[2])
            return ap

    A0 = Arena(0)
    PAR = A0.alloc([128, np_cols], F32, "par")
    b_par = Buf("par")
    IDB = A0.alloc([128, 128], BF16, "identbf")
    ONB = A0.alloc([128, 128], BF16, "onesbf")
    b_cst = Buf("cst")
    PERSIST_END = A0.off

    def P(name, j0=None, j1=None):
        o, n = poff[name]
        if j0 is None:
            return PAR[:, o:o + n]
        return PAR[:, o + j0:o + (j1 if j1 is not None else j0 + 1)]

    IDF = P("ident")

    def dma(q, out_ap, in_ap, reads, writes, key, slow=False):
        if slow:
            sch.op(q, lambda e, o=out_ap, i=in_ap: e.dma_start(out=o, in_=i, allow_slow_non_contiguous=True),
                   reads=reads, writes=writes, dkey=key)
        else:
            sch.op(q, lambda e, o=out_ap, i=in_ap: e.dma_start(out=o, in_=i), reads=reads, writes=writes, dkey=key)

    def mm_group(ps_ap, psbuf, pairs, reads):
        def fn(e, ps_ap=ps_ap, pairs=pairs):
            n = len(pairs)
            inst = None
            for j, (l, r) in enumerate(pairs):
                inst = e.matmul(ps_ap, l, r, start=(j == 0), stop=(j == n - 1))
            return inst
        sch.op("pe", fn, reads=reads, writes=[psbuf])

    def act(out_ap, in_ap, func, reads, writes, bias=0.0, scale=1.0, accum=None):
        if accum is None:
            sch.op("act", lambda e: e.activation(out_ap, in_ap, func, bias=bias, scale=scale),
                   reads=reads, writes=writes)
        else:
            sch.op("act", lambda e: e.activation(out_ap, in_ap, func, bias=bias, scale=scale, accum_out=accum),
                   reads=reads, writes=writes)

    def dve(fn, reads, writes, eng="dve"):
        sch.op(eng, fn, reads=reads, writes=writes)

    def tt(out_ap, a, b, op, reads, writes, eng="dve"):
        sch.op(eng, lambda e: e.tensor_tensor(out_ap, a, b, op), reads=reads, writes=writes)

    def ts(out_ap, a, s1, s2, op0, op1, reads, writes, eng="dve"):
        if op1 is None:
            sch.op(eng, lambda e: e.tensor_scalar(out_ap, a, s1, None, op0), reads=reads, writes=writes)
        else:
            sch.op(eng, lambda e: e.tensor_scalar(out_ap, a, s1, s2, op0, op1), reads=reads, writes=writes)

    def stt(out_ap, a, s, b, op0, op1, reads, writes, eng="dve"):
        sch.op(eng, lambda e: e.scalar_tensor_tensor(out_ap, a, s, b, op0, op1), reads=reads, writes=writes)

    def copy(out_ap, a, reads, writes, eng="dve"):
        sch.op(eng, lambda e: e.tensor_copy(out_ap, a), reads=reads, writes=writes)

    def recip(out_ap, a, reads, writes):
        sch.op("dve", lambda e: e.reciprocal(out_ap, a), reads=reads, writes=writes)

    def memset(ap, val, writes, eng="pool"):
        sch.op(eng, lambda e: e.memset(ap, val), writes=writes)

    class WSlots:
        def __init__(self, ar, n=2, nbytes=16384):
            self.nbytes = nbytes
            self.raw = [ar.alloc([128, nbytes // 2], BF16, "wslot%d" % i) for i in range(n)]
            self.buf = [Buf("wslot%d" % i) for i in range(n)]
            self.i = 0
            self.n = n

        def load(self, wname, kc, ranges, q="sp"):
            i = self.i
            self.i = (i + 1) % self.n
            tot = sum(n for _, n in ranges)
            assert kc * tot * 2 <= self.nbytes
            view = self.raw[i][:, 0:kc * tot].rearrange("p (a b) -> p a b", a=kc)
            src = w_bf[wname].rearrange("(a p) n -> p a n", p=128)
            o = 0
            for (c0, n) in ranges:
                if wname == "ab_w_in":
                    rd = [win_grp[g] for g in range(c0 // 256, (c0 + n - 1) // 256 + 1)]
                else:
                    rd = [wbuf[wname]]
                dma(q, view[:, :, o:o + n], src[:, :, c0:c0 + n], rd, [self.buf[i]], "wslot%d" % i)
                o += n
            return view, self.buf[i]

    def rmsnorm_tile(XT, bXT, XN, bXN, RSTD, bRSTD, gname, ncol=512):
        for c in range(16):
            tt(XN[:, c, 0:ncol], XT[:, c, 0:ncol], XT[:, c, 0:ncol], ALU.mult, [bXT[c]], [bXN[c]],
               eng=("dve" if c % 2 == 0 else "pool"))
        ps, pb = nextps()
        mm_group(ps[:, 0:ncol], pb, [(ONB, XN[:, c, 0:ncol]) for c in range(16)], bXN + [b_cst])
        act(RSTD[:, 0:ncol], ps[:, 0:ncol], AF.Sqrt, [pb], [bRSTD], bias=RMS_EPS, scale=1.0 / D)
        recip(RSTD[:, 0:ncol], RSTD[:, 0:ncol], [bRSTD], [bRSTD])
        for c in range(16):
            stt(XN[:, c, 0:ncol], XT[:, c, 0:ncol], P(gname, c), RSTD[:, 0:ncol], ALU.mult, ALU.mult,
                [bXT[c], bRSTD, b_par], [bXN[c]])

    dma("sp", PAR, par_in, [], [b_par], "par")
    copy(IDB, P("ident"), [b_par], [b_cst])
    copy(ONB, P("ones"), [b_par], [b_cst])
    win_grp = [Buf("W_ab_w_in_%d" % g) for g in range(20)]
    if "cast" in phases:
        for nm, (k, n) in WEIGHTS:
            if phases.get("weights") is not None and nm not in phases["weights"]:
                continue
            if nm == "ab_w_in":
                order = []
                for qtr in range(4):
                    order += [qtr, 4 + qtr]
                order += list(range(8, 20))
                for g in order:
                    dma("pool", w_bf[nm][:, g * 256:(g + 1) * 256], w_in[nm][:, g * 256:(g + 1) * 256], [],
                        [win_grp[g]], "cast_in%d" % g)
                continue
            rows = max(128, (2 * 1024 * 1024) // (n * 4) // 128 * 128)
            for r0 in range(0, k, rows):
                r1 = min(k, r0 + rows)
                dma("pool", w_bf[nm][r0:r1, :], w_in[nm][r0:r1, :], [], [wbuf[nm]], "cast_" + nm)

    def phase_A0():
        ar = Arena(PERSIST_END)
        ws = WSlots(ar)
        XIN = ar.alloc([128, 4, 2048], F32, "xin")
        bXIN = Buf("xin")
        XT = ar.alloc([128, 16, 512], F32, "xt")
        bXT = [Buf("xt%d" % c) for c in range(16)]
        XN = ar.alloc([128, 16, 512], BF16, "xn")
        bXN = [Buf("xn%d" % c) for c in range(16)]
        RSTD = ar.alloc([128, 512], F32, "rstd")
        bRSTD = Buf("rstd")
        SIG = [ar.alloc([128, 512], F32, "sig%d" % i) for i in range(2)]
        bSIG = [Buf("sig%d" % i) for i in range(2)]
        GST = ar.alloc([128, 8, 512], BF16, "gst")
        bGST = Buf("gst")
        QST = ar.alloc([128, 8, 512], BF16, "qst")
        bQST = Buf("qst")
        KST = ar.alloc([128, 8, 512], BF16, "kst")
        bKST = Buf("kst")
        VST = ar.alloc([128, 4, 1024], BF16, "vst")
        bVST = Buf("vst")
        for n in range(NT):
            t0 = n * T
            dma("sp", XIN, x_in[t0:t0 + T, :].rearrange("(s p) d -> p s d", p=128), [], [bXIN], "xin")
            for c in range(16):
                ps, pb = nextps()

                def fn(e, ps=ps, c=c):
                    inst = None
                    for s in range(4):
                        inst = e.transpose(ps[:, s * 128:(s + 1) * 128], XIN[:, s, c * 128:(c + 1) * 128], IDF)
                    return inst
                sch.op("pe", fn, reads=[bXIN, b_par], writes=[pb])
                act(XT[:, c, :], ps[:, :], AF.Copy, [pb], [bXT[c]])
            dma("pool", xT[:, t0:t0 + T].rearrange("(c p) t -> p c t", p=128), XT, bXT, [b_xT[n]], "xt_st")
            rmsnorm_tile(XT, bXT, XN, bXN, RSTD, bRSTD, "g_mix0")
            for qtr in range(4):
                wv, wb = ws.load("ab_w_in", 16, [(qtr * 256, 256), (1024 + qtr * 256, 256)])
                for cl in range(2):
                    c = qtr * 2 + cl
                    psv, pbv = nextps()
                    mm_group(psv[:, :], pbv, [(wv[:, kc, cl * 128:(cl + 1) * 128], XN[:, kc, :]) for kc in range(16)],
                             bXN + [wb])
                    psg, pbg = nextps()
                    mm_group(psg[:, :], pbg, [(wv[:, kc, 256 + cl * 128:256 + (cl + 1) * 128], XN[:, kc, :])
                                              for kc in range(16)], bXN + [wb])
                    act(SIG[c % 2], psg[:, :], AF.Sigmoid, [pbg], [bSIG[c % 2]])
                    tt(GST[:, c, :], psv[:, :], SIG[c % 2], ALU.mult, [pbv, bSIG[c % 2]], [bGST])
            dma("pool", gluT[:, t0:t0 + T].rearrange("(c p) t -> p c t", p=128), GST, [bGST], [b_gluT[n]], "gst")
            for which in range(2):
                ST, bST = (QST, bQST) if which == 0 else (KST, bKST)
                for h in range(8):
                    if h % 4 == 0:
                        wv, wb = ws.load("ab_w_in", 16, [(2048 + which * 1024 + h * 128, 512)])
                    ps, pb = nextps()
                    mm_group(ps[:, :], pb, [(wv[:, kc, (h % 4) * 128:(h % 4 + 1) * 128], XN[:, kc, :]) for kc in range(16)],
                             bXN + [wb])
                    if which == 0:
                        act(ST[:, h, :], ps[:, :], AF.Copy, [pb], [bST], scale=0.125)
                    else:
                        copy(ST[:, h, :], ps[:, :], [pb], [bST])
                dst = qT if which == 0 else kT
                dbuf = b_qT if which == 0 else b_kT
                dma("pool", dst[:, t0:t0 + T].rearrange("(c p) t -> p c t", p=128), ST, [bST], [dbuf[n]],
                    "qst" if which == 0 else "kst")
            for half in range(2):
                wv, wb = ws.load("ab_w_in", 16, [(4096 + half * 512, 512)])
                for s in range(4):
                    ps, pb = nextps()
                    mm_group(ps[:, :], pb, [(XN[:, kc, s * 128:(s + 1) * 128], wv[:, kc, :])
                                            for kc in range(16)], bXN + [wb])
                    if half == 0:
                        act(VST[:, s, 0:512], ps[:, :], AF.Copy, [pb], [bVST])
                    else:
                        copy(VST[:, s, 512:1024], ps[:, :], [pb], [bVST])
            dma("pool", vtok[t0:t0 + T, :].rearrange("(s p) e -> p s e", p=128), VST, [bVST], [b_v[n]], "vst")
        sch.barrier(skip=("cast",))

    def phase_B0():
        ar = Arena(PERSIST_END)
        DIAG = ar.alloc([128, 8, 31, 128], BF16, "diag")
        bDIAG = Buf("diag")
        GLUP = [ar.alloc([128, 8, 542], BF16, "glup%d" % i) for i in range(2)]
        bGLUP = [Buf("glup%d" % i) for i in range(2)]
        CV = ar.alloc([128, 8, 512], F32, "cv")
        bCV = [Buf("cv%d" % c) for c in range(8)]
        CVB = ar.alloc([128, 8, 512], BF16, "cvb")
        bCVB = [Buf("cvb%d" % c) for c in range(8)]
        CSQ = ar.alloc([128, 8, 512], BF16, "csq")
        bCSQ = [Buf("csq%d" % c) for c in range(8)]
        MEAN = ar.alloc([128, 512], F32, "mean")
        bMEAN = Buf("mean")
        VAR = ar.alloc([128, 512], F32, "var")
        bVAR = Buf("var")
        TMP = [ar.alloc([128, 512], F32, "tmp%d" % i) for i in range(2)]
        bTMP = [Buf("tmp%d" % i) for i in range(2)]
        AOUT = ar.alloc([128, 8, 512], BF16, "aout")
        bAOUT = Buf("aout")
        for c in range(8):
            for tap in range(31):
                ts(DIAG[:, c, tap, :], IDB, P("conv_dw", c * 31 + tap), None, ALU.mult, None, [b_cst, b_par], [bDIAG],
                   eng=("dve" if tap % 2 == 0 else "pool"))
        for n in range(NT):
            j = n % (S // T)
            t0 = n * T
            G = GLUP[n % 2]
            bG = bGLUP[n % 2]
            lo = 15 if j == 0 else 0
            hi = 542 - 15 if j == (S // T) - 1 else 542
            if lo > 0:
                memset(G[:, :, 0:15], 0.0, [bG])
            if hi < 542:
                memset(G[:, :, 527:542], 0.0, [bG])
            rd = [b_gluT[n]]
            if j > 0:
                rd.append(b_gluT[n - 1])
            if j < (S // T) - 1:
                rd.append(b_gluT[n + 1])
            dma("sp", G[:, :, lo:hi], gluT[:, t0 - 15 + lo:t0 - 15 + hi].rearrange("(c p) t -> p c t", p=128),
                rd, [bG], "glup%d" % (n % 2))
            for c in range(8):
                ps, pb = nextps()
                mm_group(ps[:, :], pb, [(DIAG[:, c, tap, :], G[:, c, tap:tap + 512]) for tap in range(31)],
                         [bDIAG, bG])
                act(CV[:, c, :], ps[:, :], AF.Identity, [pb, b_par], [bCV[c]], bias=P("conv_db", c))
                copy(CVB[:, c, :], CV[:, c, :], [bCV[c]], [bCVB[c]], eng="pool")
                tt(CSQ[:, c, :], CV[:, c, :], CV[:, c, :], ALU.mult, [bCV[c]], [bCSQ[c]])
            psm, pbm = nextps()
            mm_group(psm[:, :], pbm, [(ONB, CVB[:, c, :]) for c in range(8)], bCVB + [b_cst])
            psq, pbq = nextps()
            mm_group(psq[:, :], pbq, [(ONB, CSQ[:, c, :]) for c in range(8)], bCSQ + [b_cst])
            ts(MEAN, psm[:, :], 1.0 / 1024, None, ALU.mult, None, [pbm], [bMEAN])
            tt(VAR, MEAN, MEAN, ALU.mult, [bMEAN], [bVAR])
            stt(VAR, psq[:, :], 1.0 / 1024, VAR, ALU.mult, ALU.subtract, [pbq, bVAR], [bVAR])
            act(VAR, VAR, AF.Sqrt, [bVAR], [bVAR], bias=LN_EPS, scale=1.0)
            recip(VAR, VAR, [bVAR], [bVAR])
            for c in range(8):
                tm = TMP[c % 2]
                bt = bTMP[c % 2]
                tt(tm, CV[:, c, :], MEAN, ALU.subtract, [bCV[c], bMEAN], [bt])
                tt(tm, tm, VAR, ALU.mult, [bt, bVAR], [bt])
                act(AOUT[:, c, :], tm, AF.Silu, [bt, b_par], [bAOUT], bias=P("conv_ln_b", c), scale=P("conv_ln_g", c))
            dma("pool", catT[0:1024, t0:t0 + T].rearrange("(c p) t -> p c t", p=128), AOUT, [bAOUT], [b_catA[n]],
                "aout")
        sch.barrier(skip=("cast",))

    def phase_C0():
        lambda_init = 0.8 - 0.6 * math.exp(-0.3 * 0)
        ar = Arena(PERSIST_END)
        KT = [ar.alloc([128, S], BF16, "kt%d" % i) for i in range(3)]
        QT = [ar.alloc([128, S], BF16, "qt%d" % i) for i in range(3)]
        V = [ar.alloc([128, 16, 128], BF16, "v%d" % i) for i in range(3)]
        bKQV = [Buf("kqv%d" % i) for i in range(3)]
        PT = [ar.alloc([128, 16, 512], BF16, "pt%d" % i) for i in range(2)]
        bPT = [[Buf("pt%d_%d" % (i, k)) for k in range(16)] for i in range(2)]
        TMP = [ar.alloc([128, 512], F32, "tmp%d" % i) for i in range(3)]
        bTMP = [Buf("tmp%d" % i) for i in range(3)]
        R = [ar.alloc([128, 512], F32, "r%d" % i) for i in range(2)]
        bR = [Buf("r%d" % i) for i in range(2)]
        O = [ar.alloc([128, 512], F32, "o%d" % i) for i in range(2)]
        bO = [Buf("o%d" % i) for i in range(2)]
        OD = ar.alloc([128, 512], F32, "od")
        bOD = Buf("od")
        OSQ = ar.alloc([128, 512], BF16, "osq")
        bOSQ = Buf("osq")
        RR = ar.alloc([128, 512], F32, "rr")
        bRR = Buf("rr")
        BST = [ar.alloc([128, S], BF16, "bst%d" % i) for i in range(2)]
        bBST = [Buf("bst%d" % i) for i in range(2)]
        SC = ar.alloc([128, 8], F32, "sc")
        bSC = Buf("sc")
        LT = ar.alloc([128, 64], F32, "lt")
        bLT = Buf("lt")
        for i, (a, b) in enumerate((("diff_lq1", "diff_lk1"), ("diff_lq2", "diff_lk2"))):
            tt(LT, P(a), P(b), ALU.mult, [b_par], [bLT])
            dve(lambda e, i=i: e.reduce_sum(SC[:, i:i + 1], LT, AX.X), [bLT], [bSC])
            act(SC[:, i:i + 1], SC[:, i:i + 1], AF.Exp, [bSC], [bSC])
        tt(SC[:, 2:3], SC[:, 1:2], SC[:, 0:1], ALU.subtract, [bSC], [bSC])
        ts(SC[:, 3:4], SC[:, 2:3], -lambda_init, None, ALU.add, None, [bSC], [bSC])
        ts(SC[:, 4:5], P("subln_g"), 1.0 - lambda_init, None, ALU.mult, None, [bSC, b_par], [bSC])
        NEGLAM = SC[:, 3:4]
        SUBG = SC[:, 4:5]
        TD = P("tdist")
        heads = [(q, h) for q in range(NSEQ) for h in range(8)]
        stages = [(hi, j, c) for hi in range(len(heads)) for j in range(S // T) for c in range(2)]
        qk_i = [0]
        tmp_i = [0]

        def load_head(hi):
            q, h = heads[hi]
            sl = hi % 3
            c0 = q * S
            tiles = list(range(q * (S // T), (q + 1) * (S // T)))
            dma("sp", KT[sl], kT[h * 128:(h + 1) * 128, c0:c0 + S], [b_kT[n] for n in tiles], [bKQV[sl]], "kqv%d" % sl)
            dma("sp", QT[sl], qT[h * 128:(h + 1) * 128, c0:c0 + S], [b_qT[n] for n in tiles], [bKQV[sl]], "kqv%d" % sl)
            dma("sp", V[sl], vtok[c0:c0 + S, h * 128:(h + 1) * 128].rearrange("(k p) e -> p k e", p=128),
                [b_v[n] for n in tiles], [bKQV[sl]], "kqv%d" % sl)

        def s1_step(st, kc):
            hi, j, c = st
            q, h = heads[hi]
            sl = hi % 3
            slope = 2.0 ** (-8.0 * (h + 1) / 8.0)
            bk = qk_i[0] % 6
            qk_i[0] += 1
            ps, pb = psum[bk], psb[bk]
            mm_group(ps[:, :], pb, [(KT[sl][64 * c:64 * c + 64, kc * 128:(kc + 1) * 128],
                                     QT[sl][64 * c:64 * c + 64, j * 512:(j + 1) * 512])], [bKQV[sl]])
            off = j * 512 - kc * 128 + 1920
            tm = TMP[tmp_i[0] % 3]
            bt = bTMP[tmp_i[0] % 3]
            tmp_i[0] += 1
            stt(tm, TD[:, off:off + 512], -slope, ps[:, :], ALU.mult, ALU.add, [b_par, pb], [bt])
            act(PT[c][:, kc, :], tm, AF.Exp, [bt], [bPT[c][kc]])

        def s2_step(st, kc):
            hi, j, c = st
            sl = hi % 3
            sch.op("pe", lambda e: e.matmul(psum[6][:, :], ONB, PT[c][:, kc, :], start=(kc == 0), stop=(kc == 15)),
                   reads=[bPT[c][kc], b_cst], writes=[psb[6]])
            sch.op("pe", lambda e: e.matmul(psum[7][:, :], V[sl][:, kc, :], PT[c][:, kc, :], start=(kc == 0), stop=(kc == 15)),
                   reads=[bPT[c][kc], bKQV[sl]], writes=[psb[7]])

        def s2_epilogue(st):
            hi, j, c = st
            q, h = heads[hi]
            sl = hi % 3
            recip(R[c], psum[6][:, :], [psb[6]], [bR[c]])
            tt(O[c], psum[7][:, :], R[c], ALU.mult, [psb[7], bR[c]], [bO[c]])
            if c == 1:
                stt(OD, O[1], NEGLAM, O[0], ALU.mult, ALU.add, [bO[0], bO[1], bSC], [bOD])
                tt(OSQ, OD, OD, ALU.mult, [bOD], [bOSQ], eng="pool")
                psr, pbr = nextps()
                mm_group(psr[:, :], pbr, [(ONB, OSQ)], [bOSQ, b_cst])
                act(RR, psr[:, :], AF.Sqrt, [pbr], [bRR], bias=LN_EPS, scale=1.0 / 128)
                recip(RR, RR, [bRR], [bRR])
                stt(BST[hi % 2][:, j * 512:(j + 1) * 512], OD, SUBG, RR, ALU.mult, ALU.mult, [bOD, bRR, bSC],
                    [bBST[hi % 2]])
                if j == S // T - 1:
                    c0 = q * S
                    dma("pool", catT[1024 + h * 128:1024 + (h + 1) * 128, c0:c0 + S], BST[hi % 2], [bBST[hi % 2]],
                        [b_catB[q]], "bst%d" % (hi % 2))

        load_head(0)
        prev = None
        for st in stages:
            hi, j, c = st
            if j == 0 and c == 0 and hi + 1 < len(heads):
                load_head(hi + 1)
            for kc in range(16):
                s1_step(st, kc)
                if prev is not None:
                    s2_step(prev, kc)
            if prev is not None:
                s2_epilogue(prev)
            prev = st
        for kc in range(16):
            s2_step(prev, kc)
        s2_epilogue(prev)
        sch.barrier(skip=("cast",))

    def phase_M(layer, KM, bKM, VM, bVM):
        ar = Arena(MEM_END)
        ws = WSlots(ar)
        MIN = ar.alloc([128, 2, 2048], F32, "min")
        bMIN = Buf("min")
        MS = ar.alloc([128, 2, 2048], F32, "ms")
        bMS = Buf("ms")
        MNT = ar.alloc([128, 16, 256], BF16, "mnt")
        bMNT = Buf("mnt")
        SS = ar.alloc([128, 4], F32, "ss")
        bSS = Buf("ss")
        for b in range(NSEQ):
            dma("sp", MIN, mem_in[b * MEM:(b + 1) * MEM, :].rearrange("(s p) d -> p s d", p=128), [], [bMIN], "min")
            for s in range(2):
                act(MS[:, s, :], MIN[:, s, :], AF.Square, [bMIN], [bMS, bSS], accum=SS[:, s:s + 1])
            act(SS[:, 0:2], SS[:, 0:2], AF.Sqrt, [bSS], [bSS], bias=RMS_EPS, scale=1.0 / D)
            recip(SS[:, 0:2], SS[:, 0:2], [bSS], [bSS])
            for s in range(2):
                ts(MS[:, s, :], MIN[:, s, :], SS[:, s:s + 1], None, ALU.mult, None, [bMIN, bSS], [bMS])
            for c in range(16):
                ps, pb = nextps()

                def fn(e, ps=ps, c=c):
                    inst = None
                    for s in range(2):
                        inst = e.transpose(ps[:, s * 128:(s + 1) * 128], MS[:, s, c * 128:(c + 1) * 128], IDF)
                    return inst
                sch.op("pe", fn, reads=[bMS, b_par], writes=[pb])
                ts(MNT[:, c, :], ps[:, 0:256], P("g_mem%d" % layer, c), None, ALU.mult, None, [pb, b_par], [bMNT])
            for g in range(4):
                wv, wb = ws.load("xa_wk%d" % layer, 16, [(g * 512, 512)])
                for cl in range(4):
                    oc = g * 4 + cl
                    ps, pb = nextps()
                    mm_group(ps[:, 0:256], pb, [(wv[:, kc, cl * 128:(cl + 1) * 128], MNT[:, kc, :]) for kc in range(16)],
                             [bMNT, wb])
                    copy(KM[b][:, oc, :], ps[:, 0:256], [pb], [bKM[b]])
            for g in range(4):
                wv, wb = ws.load("xa_wv%d" % layer, 16, [(g * 512, 512)])
                for s in range(2):
                    ps, pb = nextps()
                    mm_group(ps[:, :], pb, [(MNT[:, kc, s * 128:(s + 1) * 128], wv[:, kc, :]) for kc in range(16)],
                             [bMNT, wb])
                    act(VM[b][:, s, g * 512:(g + 1) * 512], ps[:, :], AF.Copy, [pb], [bVM[b]])
        sch.barrier(skip=("cast",))

    def phase_D(layer, mixer_in):
        ar = Arena(MEM_END)
        ws = WSlots(ar)
        XT = ar.alloc([128, 16, 512], F32, "xt")
        bXT = [Buf("xt%d" % c) for c in range(16)]
        XN = ar.alloc([128, 16, 512], BF16, "xn")
        bXN = [Buf("xn%d" % c) for c in range(16)]
        AB = ar.alloc([128, 16, 512], BF16, "ab")
        bAB = [Buf("ab%d" % c) for c in range(16)]
        AC = ar.alloc([128, 16, 512], BF16, "ac")
        bAC = [Buf("ac%d" % c) for c in range(16)]
        RSTD = ar.alloc([128, 512], F32, "rstd")
        bRSTD = Buf("rstd")
        PM = [ar.alloc([128, 512], BF16, "pm%d" % i) for i in range(4)]
        bPM = [Buf("pm%d" % i) for i in range(4)]
        R = [ar.alloc([128, 512], F32, "r%d" % i) for i in range(2)]
        bR = [Buf("r%d" % i) for i in range(2)]
        SIG = [ar.alloc([128, 512], F32, "sig%d" % i) for i in range(2)]
        bSIG = [Buf("sig%d" % i) for i in range(2)]
        for n in range(NT):
            t0 = n * T
            b = n // (S // T)
            dma("sp", XT, xT[:, t0:t0 + T].rearrange("(c p) t -> p c t", p=128), [b_xT[n]], bXT, "xt_ld")
            if mixer_in == "cat":
                dma("sp", AB, catT[:, t0:t0 + T].rearrange("(c p) t -> p c t", p=128), [b_catA[n], b_catB[b]], bAB,
                    "ab_ld")
                for g in range(4):
                    wv, wb = ws.load("ab_w_out", 16, [(g * 512, 512)])
                    for cl in range(4):
                        dc = g * 4 + cl
                        ps, pb = nextps()
                        mm_group(ps[:, :], pb, [(wv[:, kc, cl * 128:(cl + 1) * 128], AB[:, kc, :]) for kc in range(16)],
                                 bAB + [wb])
                        tt(XT[:, dc, :], XT[:, dc, :], ps[:, :], ALU.add, [pb, bXT[dc]], [bXT[dc]])
            else:
                dma("sp", AB, catT[:, t0:t0 + T].rearrange("(c p) t -> p c t", p=128), [b_catA[n]], bAB, "ab_ld")
                for g in range(8):
                    wv, wb = ws.load("s5_w_val", 16, [(g * 256, 256)])
                    wg, wgb = ws.load("s5_w_gate", 16, [(g * 256, 256)])
                    for cl in range(2):
                        dc = g * 2 + cl
                        psv, pbv = nextps()
                        mm_group(psv[:, :], pbv, [(wv[:, kc, cl * 128:(cl + 1) * 128], AB[:, kc, :]) for kc in range(16)],
                                 bAB + [wb])
                        psg, pbg = nextps()
                        mm_group(psg[:, :], pbg, [(wg[:, kc, cl * 128:(cl + 1) * 128], AB[:, kc, :]) for kc in range(16)],
                                 bAB + [wgb])
                        act(SIG[dc % 2], psg[:, :], AF.Sigmoid, [pbg], [bSIG[dc % 2]])
                        tt(SIG[dc % 2], SIG[dc % 2], psv[:, :], ALU.mult, [pbv, bSIG[dc % 2]], [bSIG[dc % 2]])
                        tt(XT[:, dc, :], XT[:, dc, :], SIG[dc % 2], ALU.add, [bSIG[dc % 2], bXT[dc]], [bXT[dc]],
                           eng="pool")
            rmsnorm_tile(XT, bXT, XN, bXN, RSTD, bRSTD, "g_xa%d" % layer)
            for g in range(4):
                wv, wb = ws.load("xa_wq%d" % layer, 16, [(g * 512, 512)])
                for cl in range(4):
                    oc = g * 4 + cl
                    ps, pb = nextps()
                    mm_group(ps[:, :], pb, [(wv[:, kc, cl * 128:(cl + 1) * 128], XN[:, kc, :]) for kc in range(16)],
                             bXN + [wb])
                    act(AB[:, oc, :], ps[:, :], AF.Copy, [pb], [bAB[oc]], scale=512.0 ** -0.5)
            for hh in range(4):
                for mc in range(2):
                    ps, pb = nextps()
                    mm_group(ps[:, :], pb, [(KM[b][:, 4 * hh + dc, mc * 128:(mc + 1) * 128], AB[:, 4 * hh + dc, :])
                                            for dc in range(4)], [bKM[b]] + bAB[4 * hh:4 * hh + 4])
                    pi = (hh % 2) * 2 + mc
                    act(PM[pi], ps[:, :], AF.Exp, [pb], [bPM[pi]])
                pis = [(hh % 2) * 2 + mc for mc in range(2)]
                pss, pbs = nextps()
                mm_group(pss[:, :], pbs, [(ONB, PM[pi]) for pi in pis], [bPM[pi] for pi in pis] + [b_cst])
                recip(R[hh % 2], pss[:, :], [pbs], [bR[hh % 2]])
                for dvc in range(4):
                    ps, pb = nextps()
                    mm_group(ps[:, :], pb, [(VM[b][:, mc, hh * 512 + dvc * 128:hh * 512 + (dvc + 1) * 128], PM[pis[mc]])
                                            for mc in range(2)], [bVM[b]] + [bPM[pi] for pi in pis])
                    tt(AC[:, 4 * hh + dvc, :], ps[:, :], R[hh % 2], ALU.mult, [pb, bR[hh % 2]], [bAC[4 * hh + dvc]])
            for g in range(4):
                wv, wb = ws.load("xa_wo%d" % layer, 16, [(g * 512, 512)])
                for cl in range(4):
                    dc = g * 4 + cl
                    ps, pb = nextps()
                    mm_group(ps[:, :], pb, [(wv[:, kc, cl * 128:(cl + 1) * 128], AC[:, kc, :]) for kc in range(16)],
                             bAC + [wb])
                    tt(XT[:, dc, :], XT[:, dc, :], ps[:, :], ALU.add, [pb, bXT[dc]], [bXT[dc]])
            dma("pool", xT[:, t0:t0 + T].rearrange("(c p) t -> p c t", p=128), XT, bXT, [b_xT[n]], "xt_st")
            rmsnorm_tile(XT, bXT, XN, bXN, RSTD, bRSTD, "g_ffn%d" % layer)
            dma("pool", xnT[:, t0:t0 + T].rearrange("(c p) t -> p c t", p=128), XN, bXN, [b_xnT[n]], "xn_st")
        sch.barrier(skip=("cast",))

    def phase_E(layer):
        ar = Arena(PERSIST_END)
        WU = [ar.alloc([128, 16, 512], BF16, "wu%d" % i) for i in range(2)]
        bWU = [Buf("wu%d" % i) for i in range(2)]
        WD = [ar.alloc([128, 44, 256], BF16, "wd%d" % i) for i in range(2)]
        bWD = [Buf("wd%d" % i) for i in range(2)]
        XN = [ar.alloc([128, 16, 512], BF16, "xn0")] * 2
        bXN = [Buf("xn0")] * 2
        HV = ar.alloc([128, 44, 512], BF16, "hv")
        bHV = [Buf("hv%d" % c) for c in range(44)]
        HX = [ar.alloc([128, 514], F32, "hx%d" % i) for i in range(4)]
        bHX = [Buf("hx%d" % i) for i in range(4)]
        OC = [ar.alloc([128, 512], F32, "oc%d" % i) for i in range(4)]
        bOC = [Buf("oc%d" % i) for i in range(4)]
        TAIL = ar.alloc([128, 88, 2], F32, "tail")
        bTAIL = [Buf("tail%d" % i) for i in range(88)]
        XT = [ar.alloc([128, 2, 512], F32, "xt%d" % i) for i in range(2)]
        bXT = [Buf("xt%d" % i) for i in range(2)]
        wname_u = "ffn_w_up%d" % layer
        wname_d = "ffn_w_down%d" % layer
        src_u = w_bf[wname_u].rearrange("(a p) n -> p a n", p=128)
        src_d = w_bf[wname_d].rearrange("(a p) n -> p a n", p=128)
        DW = lambda oc, k: P("ffn_dw%d" % layer, oc * 3 + k)
        DB = lambda oc: P("ffn_db%d" % layer, oc)
        wu_i = 0
        wd_i = 0
        hx_i = 0
        xt_i = 0
        NTS = S // T
        for oc in range(88):
            memset(TAIL[:, oc, :], 0.0, [bTAIL[oc]])

        def conv_chunk(oc, ps, pb, n, hx, bhx, ocb, bocb):
            copy(hx[:, 0:2], TAIL[:, oc, :], [bTAIL[oc]], [bhx], eng="pool")
            act(hx[:, 2:514], ps[:, :], AF.Copy, [pb], [bhx])
            copy(TAIL[:, oc, :], hx[:, 512:514], [bhx], [bTAIL[oc]], eng="pool")
            act(ocb[:, :], hx[:, 1:513], AF.Identity, [bhx, b_par], [bocb], bias=DB(oc), scale=DW(oc, 1))
            first_of_seq = (n % NTS == 0)
            if first_of_seq and n > 0:
                stt(ocb[:, 0:1], hx[:, 0:1], DW(oc, 0), ocb[:, 0:1], ALU.mult, ALU.add, [bhx, bocb, b_par], [bocb])
                stt(ocb[:, 2:512], hx[:, 2:512], DW(oc, 0), ocb[:, 2:512], ALU.mult, ALU.add, [bhx, bocb, b_par], [bocb])
                stt(ocb[:, 1:512], hx[:, 3:514], DW(oc, 2), ocb[:, 1:512], ALU.mult, ALU.add, [bhx, bocb, b_par], [bocb])
            else:
                stt(ocb[:, :], hx[:, 0:512], DW(oc, 0), ocb[:, :], ALU.mult, ALU.add, [bhx, bocb, b_par], [bocb])
                stt(ocb[:, :], hx[:, 2:514], DW(oc, 2), ocb[:, :], ALU.mult, ALU.add, [bhx, bocb, b_par], [bocb])

        for n in range(NT + 1):
            last = (n == NT)
            t0 = n * T
            lo = 1 if n == 0 else 0
            ncol = 1 if last else 512
            if not last:
                X = XN[n % 2]
                bX = bXN[n % 2]
                dma("sp", X, xnT[:, t0:t0 + T].rearrange("(c p) t -> p c t", p=128), [b_xnT[n]], [bX], "xn_ld")
                for g in range(22):
                    W = WU[wu_i % 2]
                    bW = bWU[wu_i % 2]
                    key = "wu%d" % (wu_i % 2)
                    wu_i += 1
                    dma("sp", W[:, :, 0:256], src_u[:, :, g * 256:(g + 1) * 256], [wbuf[wname_u]], [bW], key)
                    dma("sp", W[:, :, 256:512], src_u[:, :, DFF + g * 256:DFF + (g + 1) * 256], [wbuf[wname_u]], [bW], key)
                    for cl in range(2):
                        c = g * 2 + cl
                        outs = []
                        for part in range(2):
                            oc = c + 44 * part
                            ps, pb = nextps()
                            mm_group(ps[:, :], pb, [(W[:, kc, part * 256 + cl * 128:part * 256 + (cl + 1) * 128], X[:, kc, :])
                                                    for kc in range(16)], [bX, bW])
                            hx = HX[hx_i % 4]
                            bhx = bHX[hx_i % 4]
                            ocb = OC[hx_i % 4]
                            bocb = bOC[hx_i % 4]
                            hx_i += 1
                            conv_chunk(oc, ps, pb, n, hx, bhx, ocb, bocb)
                            outs.append((ocb, bocb))
                        (og, bog), (ov, bov) = outs
                        act(og, og, AF.Silu, [bog], [bog])
                        tt(HV[:, c, :], og, ov, ALU.mult, [bog, bov], [bHV[c]], eng="pool")
            else:
                for c in range(44):
                    outs = []
                    for part in range(2):
                        oc = c + 44 * part
                        ocb = OC[hx_i % 4]
                        bocb = bOC[hx_i % 4]
                        hx_i += 1
                        act(ocb[:, 0:1], TAIL[:, oc, 1:2], AF.Identity, [bTAIL[oc], b_par], [bocb], bias=DB(oc),
                            scale=DW(oc, 1))
                        stt(ocb[:, 0:1], TAIL[:, oc, 0:1], DW(oc, 0), ocb[:, 0:1], ALU.mult, ALU.add,
                            [bTAIL[oc], bocb, b_par], [bocb])
                        outs.append((ocb, bocb))
                    (og, bog), (ov, bov) = outs
                    act(og[:, 0:1], og[:, 0:1], AF.Silu, [bog], [bog])
                    tt(HV[:, c, 0:1], og[:, 0:1], ov[:, 0:1], ALU.mult, [bog, bov], [bHV[c]], eng="pool")
            tok0 = t0 - 1 + lo
            nv = ncol - lo
            rd_x = [b_xT[min(n, NT - 1)]] + ([b_xT[n - 1]] if n > 0 else [])
            for g in range(8):
                W = WD[wd_i % 2]
                bW = bWD[wd_i % 2]
                key = "wd%d" % (wd_i % 2)
                wd_i += 1
                dma("sp", W, src_d[:, :, g * 256:(g + 1) * 256], [wbuf[wname_d]], [bW], key)
                XTt = XT[xt_i % 2]
                bXTt = bXT[xt_i % 2]
                xkey = "xte%d" % (xt_i % 2)
                xt_i += 1
                dma("sp", XTt[:, :, 0:nv],
                    xT[g * 256:(g + 1) * 256, tok0:tok0 + nv].rearrange("(c p) t -> p c t", p=128),
                    rd_x, [bXTt], xkey, slow=(nv == 1))
                for cl in range(2):
                    ps, pb = nextps()
                    mm_group(ps[:, 0:nv], pb, [(W[:, fc, cl * 128:(cl + 1) * 128], HV[:, fc, lo:ncol]) for fc in range(44)],
                             bHV + [bW])
                    tt(XTt[:, cl, 0:nv], XTt[:, cl, 0:nv], ps[:, 0:nv], ALU.add, [pb, bXTt], [bXTt])
                dma("pool", xT[g * 256:(g + 1) * 256, tok0:tok0 + nv].rearrange("(c p) t -> p c t", p=128),
                    XTt[:, :, 0:nv], [bXTt], [b_xT[min(n, NT - 1)], b_xT[max(n - 1, 0)]], xkey, slow=(nv == 1))
        sch.barrier(skip=("cast",))

    def phase_N1():
        ar = Arena(PERSIST_END)
        XT2 = [ar.alloc([128, 16, 512], F32, "xt%d" % i) for i in range(2)]
        bXT2 = [[Buf("xt%d_%d" % (i, c)) for c in range(16)] for i in range(2)]
        XN2 = [ar.alloc([128, 16, 512], BF16, "xn%d" % i) for i in range(2)]
        bXN2 = [[Buf("xn%d_%d" % (i, c)) for c in range(16)] for i in range(2)]
        RSTD2 = [ar.alloc([128, 512], F32, "rstd%d" % i) for i in range(2)]
        bRSTD2 = [Buf("rstd%d" % i) for i in range(2)]
        for n in range(NT):
            t0 = n * T
            XT, bXT, XN, bXN, RSTD, bRSTD = XT2[n % 2], bXT2[n % 2], XN2[n % 2], bXN2[n % 2], RSTD2[n % 2], bRSTD2[n % 2]
            dma("sp", XT, xT[:, t0:t0 + T].rearrange("(c p) t -> p c t", p=128), [b_xT[n], b_xT[min(n + 1, NT - 1)]],
                bXT, "xt_ld%d" % (n % 2))
            rmsnorm_tile(XT, bXT, XN, bXN, RSTD, bRSTD, "g_mix1")
            dma("pool", xnT[:, t0:t0 + T].rearrange("(c p) t -> p c t", p=128), XN, bXN, [b_xnT[n]], "xn_st%d" % (n % 2))
        sch.barrier(skip=("cast",))

    def phase_S5():
        I32 = mybir.dt.int32
        ar = Arena(PERSIST_END)
        cnt = [0]

        def new(shape=(128, 256), dt=F32):
            cnt[0] += 1
            return ar.alloc(list(shape), dt, "s5t%d" % cnt[0]), Buf("s5t%d" % cnt[0])

        SP3, bSP3 = new((128, 768))
        dma("sp", SP3, s5par[:, 0:768], [], [bSP3], "s5p")
        LR = SP3[:, 0:256]
        LI = SP3[:, 256:512]
        LDT = SP3[:, 512:768]
        DT, bDT = new()
        act(DT, LDT, AF.Exp, [bSP3], [bDT])
        Z, bZ = new()
        tt(Z, LR, DT, ALU.mult, [bSP3, bDT], [bZ])
        MAG, bMAG = new()
        ce = [1.0 / math.factorial(i) for i in range(7)]
        ts(MAG, Z, ce[6], ce[5], ALU.mult, ALU.add, [bZ], [bMAG])
        for i in (4, 3, 2, 1, 0):
            tt(MAG, MAG, Z, ALU.mult, [bMAG, bZ], [bMAG])
            ts(MAG, MAG, ce[i], None, ALU.add, None, [bMAG], [bMAG])
        ANG, bANG = new()
        tt(ANG, LI, DT, ALU.mult, [bSP3, bDT], [bANG])
        NF, bNF = new()
        NI, bNI = new((128, 256), I32)
        ts(NF, ANG, 1.0 / (2 * math.pi), None, ALU.mult, None, [bANG], [bNF])
        copy(NI, NF, [bNF], [bNI])
        copy(NF, NI, [bNI], [bNF])
        W, bW = new()
        stt(W, NF, -2.0 * math.pi, ANG, ALU.mult, ALU.add, [bNF, bANG], [bW])
        ts(W, W, 0.25, None, ALU.mult, None, [bW], [bW])
        W2, bW2 = new()
        tt(W2, W, W, ALU.mult, [bW], [bW2])
        SS_, bSS_ = new()
        CC_, bCC_ = new()
        cs = [(-1.0) ** i / math.factorial(2 * i + 1) for i in range(7)]
        cc = [(-1.0) ** i / math.factorial(2 * i) for i in range(8)]
        ts(SS_, W2, cs[6], cs[5], ALU.mult, ALU.add, [bW2], [bSS_])
        for i in (4, 3, 2, 1, 0):
            tt(SS_, SS_, W2, ALU.mult, [bSS_, bW2], [bSS_])
            ts(SS_, SS_, cs[i], None, ALU.add, None, [bSS_], [bSS_])
        tt(SS_, SS_, W, ALU.mult, [bSS_, bW], [bSS_])
        ts(CC_, W2, cc[7], cc[6], ALU.mult, ALU.add, [bW2], [bCC_])
        for i in (5, 4, 3, 2, 1, 0):
            tt(CC_, CC_, W2, ALU.mult, [bCC_, bW2], [bCC_])
            ts(CC_, CC_, cc[i], None, ALU.add, None, [bCC_], [bCC_])
        T1, bT1 = new()
        for _ in range(2):
            tt(T1, SS_, SS_, ALU.mult, [bSS_], [bT1])
            tt(SS_, SS_, CC_, ALU.mult, [bSS_, bCC_], [bSS_])
            ts(SS_, SS_, 2.0, None, ALU.mult, None, [bSS_], [bSS_])
            ts(CC_, T1, -2.0, 1.0, ALU.mult, ALU.add, [bT1], [bCC_])
        PR, bPR = new((128, 11, 256))
        PI, bPI = new((128, 11, 256))
        PSG, bPSG = new((128, 11, 256))
        tt(PR[:, 0, :], MAG, CC_, ALU.mult, [bMAG, bCC_], [bPR])
        tt(PI[:, 0, :], MAG, SS_, ALU.mult, [bMAG, bSS_], [bPI])
        DEN, bDEN = new()
        tt(DEN, LR, LR, ALU.mult, [bSP3], [bDEN])
        tt(T1, LI, LI, ALU.mult, [bSP3], [bT1])
        tt(DEN, DEN, T1, ALU.add, [bDEN, bT1], [bDEN])
        recip(DEN, DEN, [bDEN], [bDEN])
        A1, bA1 = new()
        ts(A1, PR[:, 0, :], -1.0, None, ALU.add, None, [bPR], [bA1])
        FRE, bFRE = new()
        FIM, bFIM = new()
        T2, bT2 = new()
        tt(FRE, A1, LR, ALU.mult, [bA1, bSP3], [bFRE])
        tt(T2, PI[:, 0, :], LI, ALU.mult, [bPI, bSP3], [bT2])
        tt(FRE, FRE, T2, ALU.add, [bFRE, bT2], [bFRE])
        tt(FRE, FRE, DEN, ALU.mult, [bFRE, bDEN], [bFRE])
        tt(FIM, PI[:, 0, :], LR, ALU.mult, [bPI, bSP3], [bFIM])
        tt(T2, A1, LI, ALU.mult, [bA1, bSP3], [bT2])
        tt(FIM, FIM, T2, ALU.subtract, [bFIM, bT2], [bFIM])
        tt(FIM, FIM, DEN, ALU.mult, [bFIM, bDEN], [bFIM])
        CB, bCB = new()
        ts(CB, FIM, P("sgn_tb"), None, ALU.mult, None, [bFIM, b_par], [bCB])
        for k in range(1, 11):
            tt(T1, PR[:, k - 1, :], PR[:, k - 1, :], ALU.mult, [bPR], [bT1])
            tt(T2, PI[:, k - 1, :], PI[:, k - 1, :], ALU.mult, [bPI], [bT2])
            tt(PR[:, k, :], T1, T2, ALU.subtract, [bT1, bT2], [bPR])
            tt(T1, PR[:, k - 1, :], PI[:, k - 1, :], ALU.mult, [bPR, bPI], [bT1])
            ts(PI[:, k, :], T1, 2.0, None, ALU.mult, None, [bT1], [bPI])
        for k in range(11):
            ts(PSG[:, k, :], PI[:, k, :], P("nsgn"), None, ALU.mult, None, [bPI, b_par], [bPSG])
        SHB, bSHB = new((128, 128), BF16)
        copy(SHB, P("shift64"), [b_par], [bSHB])

        XNC, bXNC = new((128, NSEQ, S), BF16)
        HB = [[(new((128, S), BF16)[0], [Buf("hbt%d_%d_%d" % (i_, d, j)) for j in range(S // T)]) for d in range(2)]
              for i_ in range(2)]
        Y, bY = new((128, NSEQ, S), F32)
        GO, bGO = new((128, NSEQ, S), BF16)
        UP, bUP = new((128, 2, 3, 128), F32)
        XALL = [new((128, 128), F32) for _ in range(2)]
        WB = [[new((128, 128), BF16) for d in range(2)] for gg in range(8)]
        WC = [[new((128, 128), BF16) for d in range(2)] for gg in range(8)]
        AKS = [[[new((128, 128), BF16) for k in range(11)] for d in range(2)] for _ in range(2)]
        TG, bTG = new((128, 512), F32)
        NJ = S // T
        hset = 0
        cast_i = 0

        def cast(out_ap, in_ap, reads, writes):
            nonlocal cast_i
            if cast_i % 2 == 0:
                act(out_ap, in_ap, AF.Copy, reads, writes)
            else:
                copy(out_ap, in_ap, reads, writes)
            cast_i += 1

        def acc_mm(ps_ap, psbuf, lhsT, rhs, reads):
            sch.op("pe", lambda e: e.matmul(ps_ap, lhsT, rhs, start=False, stop=True, skip_group_check=True),
                   reads=reads + [psbuf], writes=[psbuf])

        for ch in range(16):
            dma("sp", XNC, xnT[ch * 128:(ch + 1) * 128, :].rearrange("p (q t) -> p q t", q=NSEQ), b_xnT[0:NT], [bXNC], "xnc")
            base = 768 + ch * 2 * 3 * 128
            dma("sp", UP, s5par[:, base:base + 768].rearrange("p (d w c) -> p d w c", d=2, w=3), [], [bUP], "up")
            for d in range(2):
                XA, bXA = XALL[d]
                for gg in range(8):
                    col = d * 128 + ch * 8 + gg
                    ts(XA[:, gg * 16:(gg + 1) * 16], UP[:, d, 0, gg * 16:(gg + 1) * 16], FRE[:, col:col + 1], None,
                       ALU.mult, None, [bUP, bFRE], [bXA])
                    stt(XA[:, gg * 16:(gg + 1) * 16], UP[:, d, 1, gg * 16:(gg + 1) * 16], CB[:, col:col + 1],
                        XA[:, gg * 16:(gg + 1) * 16], ALU.mult, ALU.add, [bUP, bCB, bXA], [bXA])
                ps, pb = nextps()
                sch.op("pe", lambda e, ps=ps, XA=XA: e.transpose(ps[:, 0:128], XA, IDF), reads=[bXA, b_par], writes=[pb])
                for gg in range(8):
                    wbt, bwbt = WB[gg][d]
                    ts(wbt, ps[:, 0:128], P("gmask", gg), None, ALU.mult, None, [pb, b_par], [bwbt])
                    wct, bwct = WC[gg][d]
                    memset(wct, 0.0, [bwct])
                    ts(wct[:, gg * 16:(gg + 1) * 16], UP[:, d, 2, gg * 16:(gg + 1) * 16], P("nsgn"), None, ALU.mult, None,
                       [bUP, b_par], [bwct], eng="pool")
            for gg in range(8):
                aks = AKS[gg % 2]
                for d in range(2):
                    col = d * 128 + ch * 8 + gg
                    for k in range(11):
                        ak, bak = aks[d][k]
                        act(ak, IDB, AF.Copy, [b_cst, bPR], [bak], scale=PR[:, k, col:col + 1])
                        stt(ak, SHB, PSG[:, k, col:col + 1], ak, ALU.mult, ALU.add, [bSHB, bPSG, bak], [bak])
                for q in range(NSEQ):
                    hb = HB[hset % 2]
                    hset += 1
                    for d in range(2):
                        Hb, bHb = hb[d]
                        wbt, bwbt = WB[gg][d]
                        for j in range(NJ):
                            bk = 4 * d + j
                            mm_group(psum[bk][:, :], psb[bk], [(wbt, XNC[:, q, j * 512:(j + 1) * 512])], [bwbt, bXNC])
                            cast(Hb[:, j * 512:(j + 1) * 512], psum[bk][:, :], [psb[bk]], [bHb[j]])
                    for k in range(11):
                        sft = 1 << k
                        for d in range(2):
                            Hb, bHb = hb[d]
                            ak, bak = aks[d][k]
                            work = []
                            for j in range(NJ):
                                if d == 0:
                                    d0 = max(512 * j, sft)
                                    d1 = 512 * (j + 1)
                                    s0 = d0 - sft
                                else:
                                    d0 = 512 * j
                                    d1 = min(512 * (j + 1), S - sft)
                                    s0 = d0 + sft
                                if d0 >= d1:
                                    continue
                                w = d1 - d0
                                bk = 4 * d + j
                                srcb = [bHb[t_] for t_ in range(s0 // 512, (s0 + w - 1) // 512 + 1)]
                                acc_mm(psum[bk][:, d0 - 512 * j:d0 - 512 * j + w], psb[bk], ak, Hb[:, s0:s0 + w], [bak] + srcb)
                                work.append((bk, j, d0, w))
                            for (bk, j, d0, w) in work:
                                cast(Hb[:, d0:d0 + w], psum[bk][:, d0 - 512 * j:d0 - 512 * j + w], [psb[bk]], [bHb[j]])
                    for j in range(NJ):
                        mm_group(psum[j][:, :], psb[j], [(WC[gg][0][0], hb[0][0][:, j * 512:(j + 1) * 512]),
                                                         (WC[gg][1][0], hb[1][0][:, j * 512:(j + 1) * 512])],
                                 [WC[gg][0][1], WC[gg][1][1], hb[0][1][j], hb[1][1][j]])
                        if gg == 0:
                            copy(Y[:, q, j * 512:(j + 1) * 512], psum[j][:, :], [psb[j]], [bY])
                        else:
                            tt(Y[:, q, j * 512:(j + 1) * 512], Y[:, q, j * 512:(j + 1) * 512], psum[j][:, :], ALU.add,
                               [psb[j], bY], [bY])
            for q in range(NSEQ):
                for j in range(NJ):
                    sl = slice(j * 512, (j + 1) * 512)
                    stt(Y[:, q, sl], XNC[:, q, sl], P("s5_d", ch), Y[:, q, sl], ALU.mult, ALU.add, [bXNC, bY, b_par], [bY])
                    tt(TG, Y[:, q, sl], Y[:, q, sl], ALU.mult, [bY], [bTG], eng="pool")
                    ts(TG, TG, 0.044715, 1.0, ALU.mult, ALU.add, [bTG], [bTG], eng="pool")
                    tt(TG, TG, Y[:, q, sl], ALU.mult, [bTG, bY], [bTG], eng="pool")
                    act(TG, TG, AF.Sigmoid, [bTG], [bTG], scale=2.0 * math.sqrt(2.0 / math.pi))
                    tt(GO[:, q, sl], TG, Y[:, q, sl], ALU.mult, [bTG, bY], [bGO])
            dma("pool", catT[ch * 128:(ch + 1) * 128, :].rearrange("p (q t) -> p q t", q=NSEQ), GO, [bGO],
                b_catA[0:NT], "go")
        sch.barrier(skip=("cast",))

    def phase_F():
        ar = Arena(PERSIST_END)
        XT2 = [ar.alloc([128, 16, 512], F32, "xt%d" % i) for i in range(2)]
        bXT2 = [[Buf("xt%d_%d" % (i, c)) for c in range(16)] for i in range(2)]
        XN = ar.alloc([128, 16, 512], BF16, "xn")
        bXN = [Buf("xn%d" % c) for c in range(16)]
        RSTD = ar.alloc([128, 512], F32, "rstd")
        bRSTD = Buf("rstd")
        XO2 = [ar.alloc([128, 16, 512], F32, "xo%d" % i) for i in range(2)]
        bXO2 = [[Buf("xo%d_%d" % (i, c)) for c in range(16)] for i in range(2)]
        OUT = [ar.alloc([128, 2048], F32, "out%d" % i) for i in range(2)]
        bOUT = [Buf("out%d" % i) for i in range(2)]
        oi = 0
        for n in range(NT):
            t0 = n * T
            XT, bXT, XO, bXO = XT2[n % 2], bXT2[n % 2], XO2[n % 2], bXO2[n % 2]
            dma("sp", XT, xT[:, t0:t0 + T].rearrange("(c p) t -> p c t", p=128), [b_xT[n], b_xT[min(n + 1, NT - 1)]],
                bXT, "xt_ld%d" % (n % 2))
            for c in range(16):
                tt(XN[:, c, :], XT[:, c, :], XT[:, c, :], ALU.mult, [bXT[c]], [bXN[c]], eng=("dve" if c % 2 == 0 else "pool"))
            ps, pb = nextps()
            mm_group(ps[:, :], pb, [(ONB, XN[:, c, :]) for c in range(16)], bXN + [b_cst])
            act(RSTD, ps[:, :], AF.Sqrt, [pb], [bRSTD], bias=RMS_EPS, scale=1.0 / D)
            recip(RSTD, RSTD, [bRSTD], [bRSTD])
            for c in range(16):
                stt(XO[:, c, :], XT[:, c, :], P("g_final", c), RSTD, ALU.mult, ALU.mult, [bXT[c], bRSTD, b_par], [bXO[c]])
            for s in range(4):
                O = OUT[oi % 2]
                bO = bOUT[oi % 2]
                key = "out%d" % (oi % 2)
                oi += 1
                for g in range(4):
                    ps, pb = nextps()

                    def fn(e, ps=ps, g=g, s=s, XO=XO):
                        inst = None
                        for cl in range(4):
                            c = g * 4 + cl
                            inst = e.transpose(ps[:, cl * 128:(cl + 1) * 128], XO[:, c, s * 128:(s + 1) * 128], IDF)
                        return inst
                    sch.op("pe", fn, reads=bXO[g * 4:g * 4 + 4] + [b_par], writes=[pb])
                    if g % 2 == 0:
                        act(O[:, g * 512:(g + 1) * 512], ps[:, :], AF.Copy, [pb], [bO])
                    else:
                        copy(O[:, g * 512:(g + 1) * 512], ps[:, :], [pb], [bO])
                dma("pool", out[t0 + s * 128:t0 + (s + 1) * 128, :], O, [bO], [], key)
        sch.barrier(skip=("cast",))

    A1 = Arena(PERSIST_END)
    KM = [A1.alloc([128, 16, 256], BF16, "km%d" % b) for b in range(NSEQ)]
    VM = [A1.alloc([128, 2, 2048], BF16, "vm%d" % b) for b in range(NSEQ)]
    bKM = [Buf("km%d" % b) for b in range(NSEQ)]
    bVM = [Buf("vm%d" % b) for b in range(NSEQ)]
    MEM_END = A1.off

    seq = phases["seq"]
    for ph in seq:
        if ph == "A0":
            phase_A0()
        elif ph == "B0":
            phase_B0()
        elif ph == "C0":
            phase_C0()
        elif ph == "M0":
            phase_M(0, KM, bKM, VM, bVM)
        elif ph == "D0":
            phase_D(0, "cat")
        elif ph == "E0":
            phase_E(0)
        elif ph == "N1":
            phase_N1()
        elif ph == "S5":
            phase_S5()
        elif ph == "M1":
            phase_M(1, KM, bKM, VM, bVM)
        elif ph == "D1":
            phase_D(1, "s5")
        elif ph == "E1":
            phase_E(1)
        elif ph == "F":
            phase_F()
        else:
            raise ValueError(ph)
    sch.barrier()
    sch.emit(nc, es)
    es.close()
    return nc, sch


FULL_SEQ = ["A0", "B0", "C0", "M0", "D0", "E0", "N1", "S5", "M1", "D1", "E1", "F"]


def prepare_inputs(inputs):
    c = make_consts()
    c = make_params(inputs, c)
    blob = c.build()
    wmap = {
        "ab_w_in": inputs["ab_w_in"][0], "ab_w_out": inputs["ab_w_out"][0],
        "s5_w_val": inputs["s5_w_val"][0], "s5_w_gate": inputs["s5_w_gate"][0],
    }
    for l in range(2):
        for nm in ("xa_wq", "xa_wk", "xa_wv", "xa_wo", "ffn_w_up", "ffn_w_down"):
            wmap["%s%d" % (nm, l)] = inputs[nm][l]
    wmap = {k: np.ascontiguousarray(np.asarray(v, np.float32)) for k, v in wmap.items()}
    return c.off, blob, wmap, make_s5par(inputs)


def kernel(**inputs):
    poff, blob, wmap, s5p = prepare_inputs(inputs)
    phases = {"cast": True, "seq": FULL_SEQ}
    nc, sch = build_program(poff, blob.shape[1], phases)
    x = np.asarray(inputs["x"], np.float32)
    mem = np.asarray(inputs["mem"], np.float32)
    in_maps = []
    for i in range(NCORES):
        m = {"x_in": np.ascontiguousarray(x[2 * i:2 * i + 2].reshape(TOK, D)),
             "mem_in": np.ascontiguousarray(mem[2 * i:2 * i + 2].reshape(NSEQ * MEM, D)),
             "par_in": blob, "s5par": s5p}
        m.update(wmap)
        in_maps.append(m)
    res = run_bass_kernel_spmd(nc, in_maps, core_ids=list(range(NCORES)))
    outs = [np.asarray(r["out"]).reshape(NSEQ, S, D) for r in res.results]
    return np.concatenate(outs, axis=0).astype(np.float32)
```

```python
import math
import os
from contextlib import ExitStack

import numpy as np
import ml_dtypes

import concourse.bass as bass
import concourse.mybir as mybir
from concourse.bass_utils import run_bass_kernel_spmd

F32 = mybir.dt.float32
BF16 = mybir.dt.bfloat16
U8 = mybir.dt.uint8
AF = mybir.ActivationFunctionType
ALU = mybir.AluOpType
AX = mybir.AxisListType

D = 2048
S = 2048
NSEQ = 2
TOK = NSEQ * S
T = 512
NT = TOK // T
DFF = 5632
NFF = DFF // 128
MEM = 256
RMS_EPS = 1e-6
LN_EPS = 1e-5
NCORES = 8

SEM_LIMIT = 50000


class Buf:
    __slots__ = ("name", "w", "r")

    def __init__(self, name):
        self.name = name
        self.w = None
        self.r = []


class Sched:
    ENGS = ["pe", "act", "dve", "pool", "sp"]

    def __init__(self):
        self.ops = []
        self.last = {e: None for e in self.ENGS}
        self.dlast = {}

    def op(self, eng, fn, reads=(), writes=(), dkey=None, extra=()):
        i = len(self.ops)
        deps = set(extra)
        for b in reads:
            if b.w is not None:
                deps.add(b.w)
        for b in writes:
            if b.w is not None:
                deps.add(b.w)
            deps.update(b.r)
        for b in reads:
            b.r.append(i)
        for b in writes:
            b.w = i
            b.r = []
        if eng == "pe":
            deps = set(d for d in deps if self.ops[d][0] != "pe" or self.ops[d][3] is not None)
        self.ops.append([eng, fn, deps, dkey])
        self.last[eng] = i
        if dkey is not None:
            self.dlast[dkey] = i
        return i

    def barrier(self, skip=()):
        deps = [v for v in self.last.values() if v is not None] + [v for k, v in self.dlast.items()
                                                                    if not any(k.startswith(p) for p in skip)]
        for e in self.ENGS:
            self.op(e, None, extra=deps)

    def emit(self, nc, es):
        ops = self.ops
        needed = set()
        for o in ops:
            needed |= o[2]
        ecount = {e: 0 for e in self.ENGS}
        dcount = {}
        ev = {}
        semnames = set()
        for i, (eng, fn, deps, dkey) in enumerate(ops):
            if dkey is not None:
                dcount[dkey] = dcount.get(dkey, 0) + 16
                ev[i] = ("d_" + dkey, dcount[dkey], 16)
                semnames.add("d_" + dkey)
            elif i in needed and fn is not None:
                k = ecount[eng]
                ecount[eng] += 1
                nm = "e_%s_%d" % (eng, k // SEM_LIMIT)
                ev[i] = (nm, k % SEM_LIMIT + 1, 1)
                semnames.add(nm)
        sems = {}
        for nm in sorted(semnames):
            sems[nm] = es.enter_context(nc.semaphore(nm))
        self.nsems = len(sems)
        per_eng = {e: [] for e in self.ENGS}
        for i, o in enumerate(ops):
            per_eng[o[0]].append(i)

        def run_engine(eng_name, e):
            waited = {}
            for i in per_eng[eng_name]:
                _, fn, deps, dkey = ops[i]
                need = {}
                for d in deps:
                    if d not in ev:
                        continue
                    nm, val, _ = ev[d]
                    if need.get(nm, 0) < val:
                        need[nm] = val
                for nm, val in need.items():
                    if waited.get(nm, 0) < val:
                        e.wait_ge(sems[nm], val)
                        waited[nm] = val
                if fn is None:
                    continue
                inst = fn(e)
                if i in ev:
                    nm, val, inc = ev[i]
                    inst.then_inc(sems[nm], inc)

        block = es.enter_context(nc.Block())

        @block.tensor
        def _(e):
            run_engine("pe", e)

        @block.scalar
        def _(e):
            run_engine("act", e)

        @block.vector
        def _(e):
            run_engine("dve", e)

        @block.gpsimd
        def _(e):
            run_engine("pool", e)

        @block.sync
        def _(e):
            run_engine("sp", e)


def _cols(v):
    v = np.asarray(v, np.float32)
    return np.ascontiguousarray(v.reshape(-1, 128).T)


class PBlob:
    def __init__(self):
        self.parts = []
        self.off = {}
        self.n = 0

    def add(self, name, arr):
        arr = np.ascontiguousarray(np.asarray(arr, np.float32))
        assert arr.shape[0] == 128, (name, arr.shape)
        arr = arr.reshape(128, -1)
        self.off[name] = (self.n, arr.shape[1])
        self.parts.append(arr)
        self.n += arr.shape[1]

    def build(self):
        return np.ascontiguousarray(np.concatenate(self.parts, axis=1))


def make_consts():
    c = PBlob()
    c.add("ident", np.eye(128, dtype=np.float32))
    c.add("ones", np.ones((128, 128), np.float32))
    jj = np.arange(3968, dtype=np.float32)[None, :]
    pp = np.arange(128, dtype=np.float32)[:, None]
    c.add("tdist", np.abs(jj - pp - 1920.0))
    k = np.arange(128)
    c.add("sgn_tb", np.where(k < 64, -1.0, 1.0).astype(np.float32).reshape(128, 1))
    c.add("nsgn", np.where(k < 64, 1.0, -1.0).astype(np.float32).reshape(128, 1))
    c.add("gmask", (k[:, None] // 16 == np.arange(8)[None, :]).astype(np.float32))
    sh = np.zeros((128, 128), np.float32)
    sh[k, (k + 64) % 128] = 1.0
    c.add("shift64", sh)
    return c


def make_s5par(inp):
    lam_re = np.asarray(inp["s5_lam_re"][0], np.float32)
    lam_im = np.asarray(inp["s5_lam_im"][0], np.float32)
    log_dt = np.asarray(inp["s5_log_dt"][0], np.float32)
    b_re = np.asarray(inp["s5_b_re"][0], np.float32)
    b_im = np.asarray(inp["s5_b_im"][0], np.float32)
    c_re = np.asarray(inp["s5_c_re"][0], np.float32)
    c_im = np.asarray(inp["s5_c_im"][0], np.float32)
    parts = []
    def two(a):
        t = a.transpose(2, 0, 1).reshape(64, 256)
        return np.concatenate([t, t], axis=0)
    parts.append(two(lam_re))
    parts.append(two(lam_im))
    parts.append(np.broadcast_to(log_dt.reshape(1, 256), (128, 256)))
    for ch in range(16):
        for d in range(2):
            br = b_re[d, 8 * ch:8 * ch + 8].transpose(1, 0, 2).reshape(64, 128)
            bi = b_im[d, 8 * ch:8 * ch + 8].transpose(1, 0, 2).reshape(64, 128)
            cr = c_re[d, 8 * ch:8 * ch + 8].transpose(2, 0, 1).reshape(64, 128)
            ci = c_im[d, 8 * ch:8 * ch + 8].transpose(2, 0, 1).reshape(64, 128)
            parts.append(np.concatenate([br, bi], axis=0))
            parts.append(np.concatenate([bi, br], axis=0))
            parts.append(np.concatenate([cr, ci], axis=0))
    return np.ascontiguousarray(np.concatenate(parts, axis=1).astype(np.float32))


S5PAR_COLS = 768 + 16 * 2 * 3 * 128


def make_params(inp, c):
    for l in range(2):
        c.add("g_mix%d" % l, _cols(inp["norm_mix_g"][l]))
        c.add("g_xa%d" % l, _cols(inp["norm_xa_g"][l]))
        c.add("g_mem%d" % l, _cols(inp["norm_mem_g"][l]))
        c.add("g_ffn%d" % l, _cols(inp["norm_ffn_g"][l]))
        dw = np.asarray(inp["ffn_dw"][l], np.float32)
        c.add("ffn_dw%d" % l, dw.reshape(3, 88, 128).transpose(2, 1, 0))
        c.add("ffn_db%d" % l, _cols(inp["ffn_db"][l]))
    c.add("g_final", _cols(inp["final_g"]))
    cw = np.asarray(inp["conv_dw"][0], np.float32)
    c.add("conv_dw", cw.reshape(31, 8, 128).transpose(2, 1, 0))
    c.add("conv_db", _cols(inp["conv_db"][0]))
    c.add("conv_ln_g", _cols(inp["conv_ln_g"][0]))
    c.add("conv_ln_b", _cols(inp["conv_ln_b"][0]))
    for nm in ("diff_lq1", "diff_lk1", "diff_lq2", "diff_lk2"):
        c.add(nm, np.broadcast_to(np.asarray(inp[nm][0], np.float32)[None, :], (128, 64)))
    c.add("subln_g", np.asarray(inp["diff_subln_g"][0], np.float32).reshape(128, 1))
    c.add("s5_d", _cols(inp["s5_d"][0]))
    return c


WEIGHTS = [
    ("ab_w_in", (2048, 5120)),
    ("xa_wk0", (2048, 2048)), ("xa_wv0", (2048, 2048)),
    ("ab_w_out", (2048, 2048)),
    ("xa_wq0", (2048, 2048)), ("xa_wo0", (2048, 2048)),
    ("ffn_w_up0", (2048, 11264)), ("ffn_w_down0", (5632, 2048)),
    ("xa_wk1", (2048, 2048)), ("xa_wv1", (2048, 2048)),
    ("s5_w_val", (2048, 2048)),
    ("s5_w_gate", (2048, 2048)),
    ("xa_wq1", (2048, 2048)), ("xa_wo1", (2048, 2048)),
    ("ffn_w_up1", (2048, 11264)), ("ffn_w_down1", (5632, 2048)),
]


def build_program(poff, np_cols, phases, debug=()):
    nc = bass.Bass("TRN2", target_bir_lowering=False)
    es = ExitStack()
    sch = Sched()

    def dram(name, shape, dt, kind="Internal"):
        if name in debug:
            kind = "ExternalOutput"
        return nc.dram_tensor(name, list(shape), dt, kind=kind).ap()

    x_in = dram("x_in", [TOK, D], F32, "ExternalInput")
    mem_in = dram("mem_in", [NSEQ * MEM, D], F32, "ExternalInput")
    par_in = dram("par_in", [128, np_cols], F32, "ExternalInput")
    s5par = dram("s5par", [128, S5PAR_COLS], F32, "ExternalInput")
    out = dram("out", [TOK, D], F32, "ExternalOutput")
    w_in = {}
    w_bf = {}
    wbuf = {}
    for nm, (k, n) in WEIGHTS:
        w_in[nm] = dram(nm, [k, n], F32, "ExternalInput")
        w_bf[nm] = dram(nm + "_bf", [k, n], BF16)
        wbuf[nm] = Buf("W_" + nm)
    xT = dram("xT", [D, TOK], F32)
    gluT = dram("gluT", [1024, TOK], BF16)
    qT = dram("qT", [1024, TOK], BF16)
    kT = dram("kT", [1024, TOK], BF16)
    vtok = dram("vtok", [TOK, 1024], BF16)
    catT = dram("catT", [2048, TOK], BF16)
    xnT = dram("xnT", [2048, TOK], BF16)
    b_xT = [Buf("xT%d" % n) for n in range(NT + 1)]
    b_gluT = [Buf("gluT%d" % n) for n in range(NT)]
    b_qT = [Buf("qT%d" % n) for n in range(NT)]
    b_kT = [Buf("kT%d" % n) for n in range(NT)]
    b_v = [Buf("v%d" % n) for n in range(NT)]
    b_catA = [Buf("catA%d" % n) for n in range(NT)]
    b_catB = [Buf("catB%d" % n) for n in range(NSEQ)]
    b_xnT = [Buf("xnT%d" % n) for n in range(NT + 1)]

    ARENA = 196608
    arena = es.enter_context(nc.sbuf_tensor("arena", [128, ARENA], U8))
    psum = [es.enter_context(nc.psum_tensor("ps%d" % i, [128, 512], F32)) for i in range(8)]
    psb = [Buf("ps%d" % i) for i in range(8)]
    ps_i = [0]

    def nextps():
        i = ps_i[0]
        ps_i[0] = (i + 1) % 8
        return psum[i], psb[i]

    class Arena:
        def __init__(self, start=0):
            self.off = start

        def alloc(self, shape, dt, name=None):
            nelem = int(np.prod(shape[1:]))
            nbytes = nelem * (2 if dt == BF16 else 4)
            nbytes_al = (nbytes + 63) // 64 * 64
            assert self.off + nbytes_al <= ARENA, ("arena overflow", name, self.off, nbytes_al)
            ap = arena[:, self.off:self.off + nbytes].bitcast(dt)
            self.off += nbytes_al
            if len(shape) == 3:
                ap = ap.rearrange("p (a b) -> p a b", a=shape[1])
            elif len(shape) == 4:
                ap = ap.rearrange("p (a b c) -> p a b c", a=shape[1], b=shape[2])
            return ap

    A0 = Arena(0)
    PAR = A0.alloc([128, np_cols], F32, "par")
    b_par = Buf("par")
    IDB = A0.alloc([128, 128], BF16, "identbf")
    ONB = A0.alloc([128, 128], BF16, "onesbf")
    b_cst = Buf("cst")
    PERSIST_END = A0.off

    def P(name, j0=None, j1=None):
        o, n = poff[name]
        if j0 is None:
            return PAR[:, o:o + n]
        return PAR[:, o + j0:o + (j1 if j1 is not None else j0 + 1)]

    IDF = P("ident")

    def dma(q, out_ap, in_ap, reads, writes, key, slow=False):
        if slow:
            sch.op(q, lambda e, o=out_ap, i=in_ap: e.dma_start(out=o, in_=i, allow_slow_non_contiguous=True),
                   reads=reads, writes=writes, dkey=key)
        else:
            sch.op(q, lambda e, o=out_ap, i=in_ap: e.dma_start(out=o, in_=i), reads=reads, writes=writes, dkey=key)

    def mm_group(ps_ap, psbuf, pairs, reads):
        def fn(e, ps_ap=ps_ap, pairs=pairs):
            n = len(pairs)
            inst = None
            for j, (l, r) in enumerate(pairs):
                inst = e.matmul(ps_ap, l, r, start=(j == 0), stop=(j == n - 1))
            return inst
        sch.op("pe", fn, reads=reads, writes=[psbuf])

    def act(out_ap, in_ap, func, reads, writes, bias=0.0, scale=1.0, accum=None):
        if accum is None:
            sch.op("act", lambda e: e.activation(out_ap, in_ap, func, bias=bias, scale=scale),
                   reads=reads, writes=writes)
        else:
            sch.op("act", lambda e: e.activation(out_ap, in_ap, func, bias=bias, scale=scale, accum_out=accum),
                   reads=reads, writes=writes)

    def dve(fn, reads, writes, eng="dve"):
        sch.op(eng, fn, reads=reads, writes=writes)

    def tt(out_ap, a, b, op, reads, writes, eng="dve"):
        sch.op(eng, lambda e: e.tensor_tensor(out_ap, a, b, op), reads=reads, writes=writes)

    def ts(out_ap, a, s1, s2, op0, op1, reads, writes, eng="dve"):
        if op1 is None:
            sch.op(eng, lambda e: e.tensor_scalar(out_ap, a, s1, None, op0), reads=reads, writes=writes)
        else:
            sch.op(eng, lambda e: e.tensor_scalar(out_ap, a, s1, s2, op0, op1), reads=reads, writes=writes)

    def stt(out_ap, a, s, b, op0, op1, reads, writes, eng="dve"):
        sch.op(eng, lambda e: e.scalar_tensor_tensor(out_ap, a, s, b, op0, op1), reads=reads, writes=writes)

    def copy(out_ap, a, reads, writes, eng="dve"):
        sch.op(eng, lambda e: e.tensor_copy(out_ap, a), reads=reads, writes=writes)

    def recip(out_ap, a, reads, writes):
        sch.op("dve", lambda e: e.reciprocal(out_ap, a), reads=reads, writes=writes)

    def memset(ap, val, writes, eng="pool"):
        sch.op(eng, lambda e: e.memset(ap, val), writes=writes)

    class WSlots:
        def __init__(self, ar, n=2, nbytes=16384):
            self.nbytes = nbytes
            self.raw = [ar.alloc([128, nbytes // 2], BF16, "wslot%d" % i) for i in range(n)]
            self.buf = [Buf("wslot%d" % i) for i in range(n)]
            self.i = 0
            self.n = n

        def load(self, wname, kc, ranges, q="sp"):
            i = self.i
            self.i = (i + 1) % self.n
            tot = sum(n for _, n in ranges)
            assert kc * tot * 2 <= self.nbytes
            view = self.raw[i][:, 0:kc * tot].rearrange("p (a b) -> p a b", a=kc)
            src = w_bf[wname].rearrange("(a p) n -> p a n", p=128)
            o = 0
            for (c0, n) in ranges:
                if wname == "ab_w_in":
                    rd = [win_grp[g] for g in range(c0 // 256, (c0 + n - 1) // 256 + 1)]
                else:
                    ensure_cast(wname)
                    rd = [wbuf[wname]]
                dma(q, view[:, :, o:o + n], src[:, :, c0:c0 + n], rd, [self.buf[i]], "wslot%d" % i)
                o += n
            return view, self.buf[i]

    def rmsnorm_tile(XT, bXT, XN, bXN, RSTD, bRSTD, gname, ncol=512):
        for c in range(16):
            tt(XN[:, c, 0:ncol], XT[:, c, 0:ncol], XT[:, c, 0:ncol], ALU.mult, [bXT[c]], [bXN[c]],
               eng=("dve" if c % 2 == 0 else "pool"))
        ps, pb = nextps()
        mm_group(ps[:, 0:ncol], pb, [(ONB, XN[:, c, 0:ncol]) for c in range(16)], bXN + [b_cst])
        act(RSTD[:, 0:ncol], ps[:, 0:ncol], AF.Sqrt, [pb], [bRSTD], bias=RMS_EPS, scale=1.0 / D)
        recip(RSTD[:, 0:ncol], RSTD[:, 0:ncol], [bRSTD], [bRSTD])
        for c in range(16):
            stt(XN[:, c, 0:ncol], XT[:, c, 0:ncol], P(gname, c), RSTD[:, 0:ncol], ALU.mult, ALU.mult,
                [bXT[c], bRSTD, b_par], [bXN[c]])

    dma("sp", PAR, par_in, [], [b_par], "par")
    copy(IDB, P("ident"), [b_par], [b_cst])
    copy(ONB, P("ones"), [b_par], [b_cst])
    win_grp = [Buf("W_ab_w_in_%d" % g) for g in range(20)]
    cast_queue = []

    def emit_casts(k):
        for _ in range(min(k, len(cast_queue))):
            nm, r0, r1 = cast_queue.pop(0)
            dma("pool", w_bf[nm][r0:r1, :], w_in[nm][r0:r1, :], [], [wbuf[nm]], "cast_" + nm)

    def ensure_cast(nm):
        while any(c[0] == nm for c in cast_queue):
            emit_casts(1)
    if "cast" in phases:
        for nm, (k, n) in WEIGHTS:
            if phases.get("weights") is not None and nm not in phases["weights"]:
                continue
            if nm == "ab_w_in":
                order = []
                for qtr in range(4):
                    order += [qtr, 4 + qtr]
                order += list(range(8, 20))
                for g in order:
                    dma("pool", w_bf[nm][:, g * 256:(g + 1) * 256], w_in[nm][:, g * 256:(g + 1) * 256], [],
                        [win_grp[g]], "cast_in%d" % g)
                continue
            rows = max(128, (2 * 1024 * 1024) // (n * 4) // 128 * 128)
            for r0 in range(0, k, rows):
                r1 = min(k, r0 + rows)
                cast_queue.append((nm, r0, r1))

    def phase_A0():
        ar = Arena(PERSIST_END)
        ws = WSlots(ar)
        XIN = ar.alloc([128, 4, 2048], F32, "xin")
        bXIN = Buf("xin")
        XT = ar.alloc([128, 16, 512], F32, "xt")
        bXT = [Buf("xt%d" % c) for c in range(16)]
        XN = ar.alloc([128, 16, 512], BF16, "xn")
        bXN = [Buf("xn%d" % c) for c in range(16)]
        RSTD = ar.alloc([128, 512], F32, "rstd")
        bRSTD = Buf("rstd")
        SIG = [ar.alloc([128, 512], F32, "sig%d" % i) for i in range(2)]
        bSIG = [Buf("sig%d" % i) for i in range(2)]
        GST = ar.alloc([128, 8, 512], BF16, "gst")
        bGST = Buf("gst")
        QST = ar.alloc([128, 8, 512], BF16, "qst")
        bQST = Buf("qst")
        KST = ar.alloc([128, 8, 512], BF16, "kst")
        bKST = Buf("kst")
        VST = ar.alloc([128, 4, 1024], BF16, "vst")
        bVST = Buf("vst")
        for n in range(NT):
            t0 = n * T
            emit_casts(8)
            dma("sp", XIN, x_in[t0:t0 + T, :].rearrange("(s p) d -> p s d", p=128), [], [bXIN], "xin")
            for c in range(16):
                ps, pb = nextps()

                def fn(e, ps=ps, c=c):
                    inst = None
                    for s in range(4):
                        inst = e.transpose(ps[:, s * 128:(s + 1) * 128], XIN[:, s, c * 128:(c + 1) * 128], IDF)
                    return inst
                sch.op("pe", fn, reads=[bXIN, b_par], writes=[pb])
                act(XT[:, c, :], ps[:, :], AF.Copy, [pb], [bXT[c]])
            dma("pool", xT[:, t0:t0 + T].rearrange("(c p) t -> p c t", p=128), XT, bXT, [b_xT[n]], "xt_st")
            rmsnorm_tile(XT, bXT, XN, bXN, RSTD, bRSTD, "g_mix0")
            for qtr in range(4):
                wv, wb = ws.load("ab_w_in", 16, [(qtr * 256, 256), (1024 + qtr * 256, 256)])
                for cl in range(2):
                    c = qtr * 2 + cl
                    psv, pbv = nextps()
                    mm_group(psv[:, :], pbv, [(wv[:, kc, cl * 128:(cl + 1) * 128], XN[:, kc, :]) for kc in range(16)],
                             bXN + [wb])
                    psg, pbg = nextps()
                    mm_group(psg[:, :], pbg, [(wv[:, kc, 256 + cl * 128:256 + (cl + 1) * 128], XN[:, kc, :])
                                              for kc in range(16)], bXN + [wb])
                    act(SIG[c % 2], psg[:, :], AF.Sigmoid, [pbg], [bSIG[c % 2]])
                    tt(GST[:, c, :], psv[:, :], SIG[c % 2], ALU.mult, [pbv, bSIG[c % 2]], [bGST])
            dma("pool", gluT[:, t0:t0 + T].rearrange("(c p) t -> p c t", p=128), GST, [bGST], [b_gluT[n]], "gst")
            for which in range(2):
                ST, bST = (QST, bQST) if which == 0 else (KST, bKST)
                for h in range(8):
                    if h % 4 == 0:
                        wv, wb = ws.load("ab_w_in", 16, [(2048 + which * 1024 + h * 128, 512)])
                    ps, pb = nextps()
                    mm_group(ps[:, :], pb, [(wv[:, kc, (h % 4) * 128:(h % 4 + 1) * 128], XN[:, kc, :]) for kc in range(16)],
                             bXN + [wb])
                    if which == 0:
                        act(ST[:, h, :], ps[:, :], AF.Copy, [pb], [bST], scale=0.125)
                    else:
                        copy(ST[:, h, :], ps[:, :], [pb], [bST])
                dst = qT if which == 0 else kT
                dbuf = b_qT if which == 0 else b_kT
                dma("pool", dst[:, t0:t0 + T].rearrange("(c p) t -> p c t", p=128), ST, [bST], [dbuf[n]],
                    "qst" if which == 0 else "kst")
            for half in range(2):
                wv, wb = ws.load("ab_w_in", 16, [(4096 + half * 512, 512)])
                for s in range(4):
                    ps, pb = nextps()
                    mm_group(ps[:, :], pb, [(XN[:, kc, s * 128:(s + 1) * 128], wv[:, kc, :])
                                            for kc in range(16)], bXN + [wb])
                    if half == 0:
                        act(VST[:, s, 0:512], ps[:, :], AF.Copy, [pb], [bVST])
                    else:
                        copy(VST[:, s, 512:1024], ps[:, :], [pb], [bVST])
            dma("pool", vtok[t0:t0 + T, :].rearrange("(s p) e -> p s e", p=128), VST, [bVST], [b_v[n]], "vst")
        sch.barrier(skip=("cast",))

    def phase_B0():
        ar = Arena(PERSIST_END)
        DIAG = ar.alloc([128, 8, 31, 128], BF16, "diag")
        bDIAG = Buf("diag")
        GLUP = [ar.alloc([128, 8, 542], BF16, "glup%d" % i) for i in range(2)]
        bGLUP = [Buf("glup%d" % i) for i in range(2)]
        CV = ar.alloc([128, 8, 512], F32, "cv")
        bCV = [Buf("cv%d" % c) for c in range(8)]
        CVB = ar.alloc([128, 8, 512], BF16, "cvb")
        bCVB = [Buf("cvb%d" % c) for c in range(8)]
        CSQ = ar.alloc([128, 8, 512], BF16, "csq")
        bCSQ = [Buf("csq%d" % c) for c in range(8)]
        MEAN = ar.alloc([128, 512], F32, "mean")
        bMEAN = Buf("mean")
        VAR = ar.alloc([128, 512], F32, "var")
        bVAR = Buf("var")
        TMP = [ar.alloc([128, 512], F32, "tmp%d" % i) for i in range(2)]
        bTMP = [Buf("tmp%d" % i) for i in range(2)]
        AOUT = ar.alloc([128, 8, 512], BF16, "aout")
        bAOUT = Buf("aout")
        for c in range(8):
            for tap in range(31):
                ts(DIAG[:, c, tap, :], IDB, P("conv_dw", c * 31 + tap), None, ALU.mult, None, [b_cst, b_par], [bDIAG],
                   eng=("dve" if tap % 2 == 0 else "pool"))
        for n in range(NT):
            j = n % (S // T)
            t0 = n * T
            emit_casts(8)
            G = GLUP[n % 2]
            bG = bGLUP[n % 2]
            lo = 15 if j == 0 else 0
            hi = 542 - 15 if j == (S // T) - 1 else 542
            if lo > 0:
                memset(G[:, :, 0:15], 0.0, [bG])
            if hi < 542:
                memset(G[:, :, 527:542], 0.0, [bG])
            rd = [b_gluT[n]]
            if j > 0:
                rd.append(b_gluT[n - 1])
            if j < (S // T) - 1:
                rd.append(b_gluT[n + 1])
            dma("sp", G[:, :, lo:hi], gluT[:, t0 - 15 + lo:t0 - 15 + hi].rearrange("(c p) t -> p c t", p=128),
                rd, [bG], "glup%d" % (n % 2))
            for c in range(8):
                ps, pb = nextps()
                mm_group(ps[:, :], pb, [(DIAG[:, c, tap, :], G[:, c, tap:tap + 512]) for tap in range(31)],
                         [bDIAG, bG])
                act(CV[:, c, :], ps[:, :], AF.Identity, [pb, b_par], [bCV[c]], bias=P("conv_db", c))
                copy(CVB[:, c, :], CV[:, c, :], [bCV[c]], [bCVB[c]], eng="pool")
                tt(CSQ[:, c, :], CV[:, c, :], CV[:, c, :], ALU.mult, [bCV[c]], [bCSQ[c]])
            psm, pbm = nextps()
            mm_group(psm[:, :], pbm, [(ONB, CVB[:, c, :]) for c in range(8)], bCVB + [b_cst])
            psq, pbq = nextps()
            mm_group(psq[:, :], pbq, [(ONB, CSQ[:, c, :]) for c in range(8)], bCSQ + [b_cst])
            ts(MEAN, psm[:, :], 1.0 / 1024, None, ALU.mult, None, [pbm], [bMEAN])
            tt(VAR, MEAN, MEAN, ALU.mult, [bMEAN], [bVAR])
            stt(VAR, psq[:, :], 1.0 / 1024, VAR, ALU.mult, ALU.subtract, [pbq, bVAR], [bVAR])
            act(VAR, VAR, AF.Sqrt, [bVAR], [bVAR], bias=LN_EPS, scale=1.0)
            recip(VAR, VAR, [bVAR], [bVAR])
            for c in range(8):
                tm = TMP[c % 2]
                bt = bTMP[c % 2]
                tt(tm, CV[:, c, :], MEAN, ALU.subtract, [bCV[c], bMEAN], [bt])
                tt(tm, tm, VAR, ALU.mult, [bt, bVAR], [bt])
                act(AOUT[:, c, :], tm, AF.Silu, [bt, b_par], [bAOUT], bias=P("conv_ln_b", c), scale=P("conv_ln_g", c))
            dma("pool", catT[0:1024, t0:t0 + T].rearrange("(c p) t -> p c t", p=128), AOUT, [bAOUT], [b_catA[n]],
                "aout")
        sch.barrier(skip=("cast",))

    def phase_C0():
        lambda_init = 0.8 - 0.6 * math.exp(-0.3 * 0)
        ar = Arena(PERSIST_END)
        KT = [ar.alloc([128, S], BF16, "kt%d" % i) for i in range(3)]
        QT = [ar.alloc([128, S], BF16, "qt%d" % i) for i in range(3)]
        V = [ar.alloc([128, 16, 128], BF16, "v%d" % i) for i in range(3)]
        bKQV = [Buf("kqv%d" % i) for i in range(3)]
        PT = [ar.alloc([128, 16, 512], BF16, "pt%d" % i) for i in range(2)]
        bPT = [[Buf("pt%d_%d" % (i, k)) for k in range(16)] for i in range(2)]
        TMP = [ar.alloc([128, 512], F32, "tmp%d" % i) for i in range(3)]
        bTMP = [Buf("tmp%d" % i) for i in range(3)]
        R = [ar.alloc([128, 512], F32, "r%d" % i) for i in range(2)]
        bR = [Buf("r%d" % i) for i in range(2)]
        O = [ar.alloc([128, 512], F32, "o%d" % i) for i in range(2)]
        bO = [Buf("o%d" % i) for i in range(2)]
        OD = ar.alloc([128, 512], F32, "od")
        bOD = Buf("od")
        OSQ = ar.alloc([128, 512], BF16, "osq")
        bOSQ = Buf("osq")
        RR = ar.alloc([128, 512], F32, "rr")
        bRR = Buf("rr")
        BST = [ar.alloc([128, S], BF16, "bst%d" % i) for i in range(2)]
        bBST = [Buf("bst%d" % i) for i in range(2)]
        SC = ar.alloc([128, 8], F32, "sc")
        bSC = Buf("sc")
        LT = ar.alloc([128, 64], F32, "lt")
        bLT = Buf("lt")
        for i, (a, b) in enumerate((("diff_lq1", "diff_lk1"), ("diff_lq2", "diff_lk2"))):
            tt(LT, P(a), P(b), ALU.mult, [b_par], [bLT])
            dve(lambda e, i=i: e.reduce_sum(SC[:, i:i + 1], LT, AX.X), [bLT], [bSC])
            act(SC[:, i:i + 1], SC[:, i:i + 1], AF.Exp, [bSC], [bSC])
        tt(SC[:, 2:3], SC[:, 1:2], SC[:, 0:1], ALU.subtract, [bSC], [bSC])
        ts(SC[:, 3:4], SC[:, 2:3], -lambda_init, None, ALU.add, None, [bSC], [bSC])
        ts(SC[:, 4:5], P("subln_g"), 1.0 - lambda_init, None, ALU.mult, None, [bSC, b_par], [bSC])
        NEGLAM = SC[:, 3:4]
        SUBG = SC[:, 4:5]
        TD = P("tdist")
        heads = [(q, h) for q in range(NSEQ) for h in range(8)]
        stages = [(hi, j, c) for hi in range(len(heads)) for j in range(S // T) for c in range(2)]
        qk_i = [0]
        tmp_i = [0]

        def load_head(hi):
            q, h = heads[hi]
            sl = hi % 3
            emit_casts(8)
            c0 = q * S
            tiles = list(range(q * (S // T), (q + 1) * (S // T)))
            dma("sp", KT[sl], kT[h * 128:(h + 1) * 128, c0:c0 + S], [b_kT[n] for n in tiles], [bKQV[sl]], "kqv%d" % sl)
            dma("sp", QT[sl], qT[h * 128:(h + 1) * 128, c0:c0 + S], [b_qT[n] for n in tiles], [bKQV[sl]], "kqv%d" % sl)
            dma("sp", V[sl], vtok[c0:c0 + S, h * 128:(h + 1) * 128].rearrange("(k p) e -> p k e", p=128),
                [b_v[n] for n in tiles], [bKQV[sl]], "kqv%d" % sl)

        def s1_step(st, kc):
            hi, j, c = st
            q, h = heads[hi]
            sl = hi % 3
            slope = 2.0 ** (-8.0 * (h + 1) / 8.0)
            bk = qk_i[0] % 6
            qk_i[0] += 1
            ps, pb = psum[bk], psb[bk]
            mm_group(ps[:, :], pb, [(KT[sl][64 * c:64 * c + 64, kc * 128:(kc + 1) * 128],
                                     QT[sl][64 * c:64 * c + 64, j * 512:(j + 1) * 512])], [bKQV[sl]])
            off = j * 512 - kc * 128 + 1920
            tm = TMP[tmp_i[0] % 3]
            bt = bTMP[tmp_i[0] % 3]
            tmp_i[0] += 1
            stt(tm, TD[:, off:off + 512], -slope, ps[:, :], ALU.mult, ALU.add, [b_par, pb], [bt])
            act(PT[c][:, kc, :], tm, AF.Exp, [bt], [bPT[c][kc]])

        def s2_step(st, kc):
            hi, j, c = st
            sl = hi % 3
            sch.op("pe", lambda e: e.matmul(psum[6][:, :], ONB, PT[c][:, kc, :], start=(kc == 0), stop=(kc == 15)),
                   reads=[bPT[c][kc], b_cst], writes=[psb[6]])
            sch.op("pe", lambda e: e.matmul(psum[7][:, :], V[sl][:, kc, :], PT[c][:, kc, :], start=(kc == 0), stop=(kc == 15)),
                   reads=[bPT[c][kc], bKQV[sl]], writes=[psb[7]])

        def s2_epilogue(st):
            hi, j, c = st
            q, h = heads[hi]
            sl = hi % 3
            recip(R[c], psum[6][:, :], [psb[6]], [bR[c]])
            tt(O[c], psum[7][:, :], R[c], ALU.mult, [psb[7], bR[c]], [bO[c]])
            if c == 1:
                stt(OD, O[1], NEGLAM, O[0], ALU.mult, ALU.add, [bO[0], bO[1], bSC], [bOD])
                tt(OSQ, OD, OD, ALU.mult, [bOD], [bOSQ], eng="pool")
                psr, pbr = nextps()
                mm_group(psr[:, :], pbr, [(ONB, OSQ)], [bOSQ, b_cst])
                act(RR, psr[:, :], AF.Sqrt, [pbr], [bRR], bias=LN_EPS, scale=1.0 / 128)
                recip(RR, RR, [bRR], [bRR])
                stt(BST[hi % 2][:, j * 512:(j + 1) * 512], OD, SUBG, RR, ALU.mult, ALU.mult, [bOD, bRR, bSC],
                    [bBST[hi % 2]])
                if j == S // T - 1:
                    c0 = q * S
                    dma("pool", catT[1024 + h * 128:1024 + (h + 1) * 128, c0:c0 + S], BST[hi % 2], [bBST[hi % 2]],
                        [b_catB[q]], "bst%d" % (hi % 2))

        load_head(0)
        prev = None
        for st in stages:
            hi, j, c = st
            if j == 0 and c == 0 and hi + 1 < len(heads):
                load_head(hi + 1)
            for kc in range(16):
                s1_step(st, kc)
                if prev is not None:
                    s2_step(prev, kc)
            if prev is not None:
                s2_epilogue(prev)
            prev = st
        for kc in range(16):
            s2_step(prev, kc)
        s2_epilogue(prev)
        sch.barrier(skip=("cast",))

    def phase_M(layer, KM, bKM, VM, bVM):
        ar = Arena(MEM_END)
        ws = WSlots(ar)
        MIN = ar.alloc([128, 2, 2048], F32, "min")
        bMIN = Buf("min")
        MS = ar.alloc([128, 2, 2048], F32, "ms")
        bMS = Buf("ms")
        MNT = ar.alloc([128, 16, 256], BF16, "mnt")
        bMNT = Buf("mnt")
        SS = ar.alloc([128, 4], F32, "ss")
        bSS = Buf("ss")
        for b in range(NSEQ):
            dma("sp", MIN, mem_in[b * MEM:(b + 1) * MEM, :].rearrange("(s p) d -> p s d", p=128), [], [bMIN], "min")
            for s in range(2):
                act(MS[:, s, :], MIN[:, s, :], AF.Square, [bMIN], [bMS, bSS], accum=SS[:, s:s + 1])
            act(SS[:, 0:2], SS[:, 0:2], AF.Sqrt, [bSS], [bSS], bias=RMS_EPS, scale=1.0 / D)
            recip(SS[:, 0:2], SS[:, 0:2], [bSS], [bSS])
            for s in range(2):
                ts(MS[:, s, :], MIN[:, s, :], SS[:, s:s + 1], None, ALU.mult, None, [bMIN, bSS], [bMS])
            for c in range(16):
                ps, pb = nextps()

                def fn(e, ps=ps, c=c):
                    inst = None
                    for s in range(2):
                        inst = e.transpose(ps[:, s * 128:(s + 1) * 128], MS[:, s, c * 128:(c + 1) * 128], IDF)
                    return inst
                sch.op("pe", fn, reads=[bMS, b_par], writes=[pb])
                ts(MNT[:, c, :], ps[:, 0:256], P("g_mem%d" % layer, c), None, ALU.mult, None, [pb, b_par], [bMNT])
            for g in range(4):
                wv, wb = ws.load("xa_wk%d" % layer, 16, [(g * 512, 512)])
                for cl in range(4):
                    oc = g * 4 + cl
                    ps, pb = nextps()
                    mm_group(ps[:, 0:256], pb, [(wv[:, kc, cl * 128:(cl + 1) * 128], MNT[:, kc, :]) for kc in range(16)],
                             [bMNT, wb])
                    copy(KM[b][:, oc, :], ps[:, 0:256], [pb], [bKM[b]])
            for g in range(4):
                wv, wb = ws.load("xa_wv%d" % layer, 16, [(g * 512, 512)])
                for s in range(2):
                    ps, pb = nextps()
                    mm_group(ps[:, :], pb, [(MNT[:, kc, s * 128:(s + 1) * 128], wv[:, kc, :]) for kc in range(16)],
                             [bMNT, wb])
                    act(VM[b][:, s, g * 512:(g + 1) * 512], ps[:, :], AF.Copy, [pb], [bVM[b]])
        sch.barrier(skip=("cast",))

    def phase_D(layer, mixer_in):
        ar = Arena(MEM_END)
        ws = WSlots(ar)
        XT = ar.alloc([128, 16, 512], F32, "xt")
        bXT = [Buf("xt%d" % c) for c in range(16)]
        XN = ar.alloc([128, 16, 512], BF16, "xn")
        bXN = [Buf("xn%d" % c) for c in range(16)]
        AB = ar.alloc([128, 16, 512], BF16, "ab")
        bAB = [Buf("ab%d" % c) for c in range(16)]
        AC = ar.alloc([128, 16, 512], BF16, "ac")
        bAC = [Buf("ac%d" % c) for c in range(16)]
        RSTD = ar.alloc([128, 512], F32, "rstd")
        bRSTD = Buf("rstd")
        PM = [ar.alloc([128, 512], BF16, "pm%d" % i) for i in range(4)]
        bPM = [Buf("pm%d" % i) for i in range(4)]
        R = [ar.alloc([128, 512], F32, "r%d" % i) for i in range(2)]
        bR = [Buf("r%d" % i) for i in range(2)]
        SIG = [ar.alloc([128, 512], F32, "sig%d" % i) for i in range(2)]
        bSIG = [Buf("sig%d" % i) for i in range(2)]
        for n in range(NT):
            t0 = n * T
            b = n // (S // T)
            dma("sp", XT, xT[:, t0:t0 + T].rearrange("(c p) t -> p c t", p=128), [b_xT[n]], bXT, "xt_ld")
            if mixer_in == "cat":
                dma("sp", AB, catT[:, t0:t0 + T].rearrange("(c p) t -> p c t", p=128), [b_catA[n], b_catB[b]], bAB,
                    "ab_ld")
                for g in range(4):
                    wv, wb = ws.load("ab_w_out", 16, [(g * 512, 512)])
                    for cl in range(4):
                        dc = g * 4 + cl
                        ps, pb = nextps()
                        mm_group(ps[:, :], pb, [(wv[:, kc, cl * 128:(cl + 1) * 128], AB[:, kc, :]) for kc in range(16)],
                                 bAB + [wb])
                        tt(XT[:, dc, :], XT[:, dc, :], ps[:, :], ALU.add, [pb, bXT[dc]], [bXT[dc]])
            else:
                dma("sp", AB, catT[:, t0:t0 + T].rearrange("(c p) t -> p c t", p=128), [b_catA[n]], bAB, "ab_ld")
                for g in range(8):
                    wv, wb = ws.load("s5_w_val", 16, [(g * 256, 256)])
                    wg, wgb = ws.load("s5_w_gate", 16, [(g * 256, 256)])
                    for cl in range(2):
                        dc = g * 2 + cl
                        psv, pbv = nextps()
                        mm_group(psv[:, :], pbv, [(wv[:, kc, cl * 128:(cl + 1) * 128], AB[:, kc, :]) for kc in range(16)],
                                 bAB + [wb])
                        psg, pbg = nextps()
                        mm_group(psg[:, :], pbg, [(wg[:, kc, cl * 128:(cl + 1) * 128], AB[:, kc, :]) for kc in range(16)],
                                 bAB + [wgb])
                        act(SIG[dc % 2], psg[:, :], AF.Sigmoid, [pbg], [bSIG[dc % 2]])
                        tt(SIG[dc % 2], SIG[dc % 2], psv[:, :], ALU.mult, [pbv, bSIG[dc % 2]], [bSIG[dc % 2]])
                        tt(XT[:, dc, :], XT[:, dc, :], SIG[dc % 2], ALU.add, [bSIG[dc % 2], bXT[dc]], [bXT[dc]],
                           eng="pool")
            rmsnorm_tile(XT, bXT, XN, bXN, RSTD, bRSTD, "g_xa%d" % layer)
            for g in range(4):
                wv, wb = ws.load("xa_wq%d" % layer, 16, [(g * 512, 512)])
                for cl in range(4):
                    oc = g * 4 + cl
                    ps, pb = nextps()
                    mm_group(ps[:, :], pb, [(wv[:, kc, cl * 128:(cl + 1) * 128], XN[:, kc, :]) for kc in range(16)],
                             bXN + [wb])
                    act(AB[:, oc, :], ps[:, :], AF.Copy, [pb], [bAB[oc]], scale=512.0 ** -0.5)
            for hh in range(4):
                for mc in range(2):
                    ps, pb = nextps()
                    mm_group(ps[:, :], pb, [(KM[b][:, 4 * hh + dc, mc * 128:(mc + 1) * 128], AB[:, 4 * hh + dc, :])
                                            for dc in range(4)], [bKM[b]] + bAB[4 * hh:4 * hh + 4])
                    pi = (hh % 2) * 2 + mc
                    act(PM[pi], ps[:, :], AF.Exp, [pb], [bPM[pi]])
                pis = [(hh % 2) * 2 + mc for mc in range(2)]
                pss, pbs = nextps()
                mm_group(pss[:, :], pbs, [(ONB, PM[pi]) for pi in pis], [bPM[pi] for pi in pis] + [b_cst])
                recip(R[hh % 2], pss[:, :], [pbs], [bR[hh % 2]])
                for dvc in range(4):
                    ps, pb = nextps()
                    mm_group(ps[:, :], pb, [(VM[b][:, mc, hh * 512 + dvc * 128:hh * 512 + (dvc + 1) * 128], PM[pis[mc]])
                                            for mc in range(2)], [bVM[b]] + [bPM[pi] for pi in pis])
                    tt(AC[:, 4 * hh + dvc, :], ps[:, :], R[hh % 2], ALU.mult, [pb, bR[hh % 2]], [bAC[4 * hh + dvc]])
            for g in range(4):
                wv, wb = ws.load("xa_wo%d" % layer, 16, [(g * 512, 512)])
                for cl in range(4):
                    dc = g * 4 + cl
                    ps, pb = nextps()
                    mm_group(ps[:, :], pb, [(wv[:, kc, cl * 128:(cl + 1) * 128], AC[:, kc, :]) for kc in range(16)],
                             bAC + [wb])
                    tt(XT[:, dc, :], XT[:, dc, :], ps[:, :], ALU.add, [pb, bXT[dc]], [bXT[dc]])
            dma("pool", xT[:, t0:t0 + T].rearrange("(c p) t -> p c t", p=128), XT, bXT, [b_xT[n]], "xt_st")
            rmsnorm_tile(XT, bXT, XN, bXN, RSTD, bRSTD, "g_ffn%d" % layer)
            dma("pool", xnT[:, t0:t0 + T].rearrange("(c p) t -> p c t", p=128), XN, bXN, [b_xnT[n]], "xn_st")
        sch.barrier(skip=("cast",))

    def phase_E(layer):
        ar = Arena(PERSIST_END)
        WU = [ar.alloc([128, 16, 512], BF16, "wu%d" % i) for i in range(2)]
        bWU = [Buf("wu%d" % i) for i in range(2)]
        WD = [ar.alloc([128, 44, 256], BF16, "wd%d" % i) for i in range(2)]
        bWD = [Buf("wd%d" % i) for i in range(2)]
        XN = [ar.alloc([128, 16, 512], BF16, "xn0")] * 2
        bXN = [Buf("xn0")] * 2
        HV = ar.alloc([128, 44, 512], BF16, "hv")
        bHV = [Buf("hv%d" % c) for c in range(44)]
        HX = [ar.alloc([128, 514], F32, "hx%d" % i) for i in range(4)]
        bHX = [Buf("hx%d" % i) for i in range(4)]
        OC = [ar.alloc([128, 512], F32, "oc%d" % i) for i in range(4)]
        bOC = [Buf("oc%d" % i) for i in range(4)]
        TAIL = ar.alloc([128, 88, 2], F32, "tail")
        bTAIL = [Buf("tail%d" % i) for i in range(88)]
        XT = [ar.alloc([128, 2, 512], F32, "xt%d" % i) for i in range(2)]
        bXT = [Buf("xt%d" % i) for i in range(2)]
        wname_u = "ffn_w_up%d" % layer
        wname_d = "ffn_w_down%d" % layer
        ensure_cast(wname_u)
        ensure_cast(wname_d)
        src_u = w_bf[wname_u].rearrange("(a p) n -> p a n", p=128)
        src_d = w_bf[wname_d].rearrange("(a p) n -> p a n", p=128)
        DW = lambda oc, k: P("ffn_dw%d" % layer, oc * 3 + k)
        DB = lambda oc: P("ffn_db%d" % layer, oc)
        wu_i = 0
        wd_i = 0
        hx_i = 0
        xt_i = 0
        NTS = S // T
        for oc in range(88):
            memset(TAIL[:, oc, :], 0.0, [bTAIL[oc]])

        def conv_chunk(oc, ps, pb, n, hx, bhx, ocb, bocb):
            copy(hx[:, 0:2], TAIL[:, oc, :], [bTAIL[oc]], [bhx], eng="pool")
            act(hx[:, 2:514], ps[:, :], AF.Copy, [pb], [bhx])
            copy(TAIL[:, oc, :], hx[:, 512:514], [bhx], [bTAIL[oc]], eng="pool")
            act(ocb[:, :], hx[:, 1:513], AF.Identity, [bhx, b_par], [bocb], bias=DB(oc), scale=DW(oc, 1))
            first_of_seq = (n % NTS == 0)
            if first_of_seq and n > 0:
                stt(ocb[:, 0:1], hx[:, 0:1], DW(oc, 0), ocb[:, 0:1], ALU.mult, ALU.add, [bhx, bocb, b_par], [bocb])
                stt(ocb[:, 2:512], hx[:, 2:512], DW(oc, 0), ocb[:, 2:512], ALU.mult, ALU.add, [bhx, bocb, b_par], [bocb])
                stt(ocb[:, 1:512], hx[:, 3:514], DW(oc, 2), ocb[:, 1:512], ALU.mult, ALU.add, [bhx, bocb, b_par], [bocb])
            else:
                stt(ocb[:, :], hx[:, 0:512], DW(oc, 0), ocb[:, :], ALU.mult, ALU.add, [bhx, bocb, b_par], [bocb])
                stt(ocb[:, :], hx[:, 2:514], DW(oc, 2), ocb[:, :], ALU.mult, ALU.add, [bhx, bocb, b_par], [bocb])

        for n in range(NT + 1):
            last = (n == NT)
            t0 = n * T
            lo = 1 if n == 0 else 0
            ncol = 1 if last else 512
            if not last:
                X = XN[n % 2]
                bX = bXN[n % 2]
                dma("sp", X, xnT[:, t0:t0 + T].rearrange("(c p) t -> p c t", p=128), [b_xnT[n]], [bX], "xn_ld")
                for g in range(22):
                    W = WU[wu_i % 2]
                    bW = bWU[wu_i % 2]
                    key = "wu%d" % (wu_i % 2)
                    wu_i += 1
                    dma("sp", W[:, :, 0:256], src_u[:, :, g * 256:(g + 1) * 256], [wbuf[wname_u]], [bW], key)
                    dma("sp", W[:, :, 256:512], src_u[:, :, DFF + g * 256:DFF + (g + 1) * 256], [wbuf[wname_u]], [bW], key)
                    for cl in range(2):
                        c = g * 2 + cl
                        outs = []
                        for part in range(2):
                            oc = c + 44 * part
                            ps, pb = nextps()
                            mm_group(ps[:, :], pb, [(W[:, kc, part * 256 + cl * 128:part * 256 + (cl + 1) * 128], X[:, kc, :])
                                                    for kc in range(16)], [bX, bW])
                            hx = HX[hx_i % 4]
                            bhx = bHX[hx_i % 4]
                            ocb = OC[hx_i % 4]
                            bocb = bOC[hx_i % 4]
                            hx_i += 1
                            conv_chunk(oc, ps, pb, n, hx, bhx, ocb, bocb)
                            outs.append((ocb, bocb))
                        (og, bog), (ov, bov) = outs
                        act(og, og, AF.Silu, [bog], [bog])
                        tt(HV[:, c, :], og, ov, ALU.mult, [bog, bov], [bHV[c]], eng="pool")
            else:
                for c in range(44):
                    outs = []
                    for part in range(2):
                        oc = c + 44 * part
                        ocb = OC[hx_i % 4]
                        bocb = bOC[hx_i % 4]
                        hx_i += 1
                        act(ocb[:, 0:1], TAIL[:, oc, 1:2], AF.Identity, [bTAIL[oc], b_par], [bocb], bias=DB(oc),
                            scale=DW(oc, 1))
                        stt(ocb[:, 0:1], TAIL[:, oc, 0:1], DW(oc, 0), ocb[:, 0:1], ALU.mult, ALU.add,
                            [bTAIL[oc], bocb, b_par], [bocb])
                        outs.append((ocb, bocb))
                    (og, bog), (ov, bov) = outs
                    act(og[:, 0:1], og[:, 0:1], AF.Silu, [bog], [bog])
                    tt(HV[:, c, 0:1], og[:, 0:1], ov[:, 0:1], ALU.mult, [bog, bov], [bHV[c]], eng="pool")
            tok0 = t0 - 1 + lo
            nv = ncol - lo
            rd_x = [b_xT[min(n, NT - 1)]] + ([b_xT[n - 1]] if n > 0 else [])
            for g in range(8):
                W = WD[wd_i % 2]
                bW = bWD[wd_i % 2]
                key = "wd%d" % (wd_i % 2)
                wd_i += 1
                dma("sp", W, src_d[:, :, g * 256:(g + 1) * 256], [wbuf[wname_d]], [bW], key)
                XTt = XT[xt_i % 2]
                bXTt = bXT[xt_i % 2]
                xkey = "xte%d" % (xt_i % 2)
                xt_i += 1
                dma("sp", XTt[:, :, 0:nv],
                    xT[g * 256:(g + 1) * 256, tok0:tok0 + nv].rearrange("(c p) t -> p c t", p=128),
                    rd_x, [bXTt], xkey, slow=(nv == 1))
                for cl in range(2):
                    ps, pb = nextps()
                    mm_group(ps[:, 0:nv], pb, [(W[:, fc, cl * 128:(cl + 1) * 128], HV[:, fc, lo:ncol]) for fc in range(44)],
                             bHV + [bW])
                    tt(XTt[:, cl, 0:nv], XTt[:, cl, 0:nv], ps[:, 0:nv], ALU.add, [pb, bXTt], [bXTt])
                dma("pool", xT[g * 256:(g + 1) * 256, tok0:tok0 + nv].rearrange("(c p) t -> p c t", p=128),
                    XTt[:, :, 0:nv], [bXTt], [b_xT[min(n, NT - 1)], b_xT[max(n - 1, 0)]], xkey, slow=(nv == 1))
        sch.barrier(skip=("cast",))

    def phase_N1():
        ar = Arena(PERSIST_END)
        XT2 = [ar.alloc([128, 16, 512], F32, "xt%d" % i) for i in range(2)]
        bXT2 = [[Buf("xt%d_%d" % (i, c)) for c in range(16)] for i in range(2)]
        XN2 = [ar.alloc([128, 16, 512], BF16, "xn%d" % i) for i in range(2)]
        bXN2 = [[Buf("xn%d_%d" % (i, c)) for c in range(16)] for i in range(2)]
        RSTD2 = [ar.alloc([128, 512], F32, "rstd%d" % i) for i in range(2)]
        bRSTD2 = [Buf("rstd%d" % i) for i in range(2)]
        for n in range(NT):
            t0 = n * T
            XT, bXT, XN, bXN, RSTD, bRSTD = XT2[n % 2], bXT2[n % 2], XN2[n % 2], bXN2[n % 2], RSTD2[n % 2], bRSTD2[n % 2]
            dma("sp", XT, xT[:, t0:t0 + T].rearrange("(c p) t -> p c t", p=128), [b_xT[n], b_xT[min(n + 1, NT - 1)]],
                bXT, "xt_ld%d" % (n % 2))
            rmsnorm_tile(XT, bXT, XN, bXN, RSTD, bRSTD, "g_mix1")
            dma("pool", xnT[:, t0:t0 + T].rearrange("(c p) t -> p c t", p=128), XN, bXN, [b_xnT[n]], "xn_st%d" % (n % 2))
        sch.barrier(skip=("cast",))

    def phase_S5():
        I32 = mybir.dt.int32
        ar = Arena(PERSIST_END)
        cnt = [0]

        def new(shape=(128, 256), dt=F32):
            cnt[0] += 1
            return ar.alloc(list(shape), dt, "s5t%d" % cnt[0]), Buf("s5t%d" % cnt[0])

        SP3, bSP3 = new((128, 768))
        dma("sp", SP3, s5par[:, 0:768], [], [bSP3], "s5p")
        LR = SP3[:, 0:256]
        LI = SP3[:, 256:512]
        LDT = SP3[:, 512:768]
        DT, bDT = new()
        act(DT, LDT, AF.Exp, [bSP3], [bDT])
        Z, bZ = new()
        tt(Z, LR, DT, ALU.mult, [bSP3, bDT], [bZ])
        MAG, bMAG = new()
        ce = [1.0 / math.factorial(i) for i in range(7)]
        ts(MAG, Z, ce[6], ce[5], ALU.mult, ALU.add, [bZ], [bMAG])
        for i in (4, 3, 2, 1, 0):
            tt(MAG, MAG, Z, ALU.mult, [bMAG, bZ], [bMAG])
            ts(MAG, MAG, ce[i], None, ALU.add, None, [bMAG], [bMAG])
        ANG, bANG = new()
        tt(ANG, LI, DT, ALU.mult, [bSP3, bDT], [bANG])
        NF, bNF = new()
        NI, bNI = new((128, 256), I32)
        ts(NF, ANG, 1.0 / (2 * math.pi), None, ALU.mult, None, [bANG], [bNF])
        copy(NI, NF, [bNF], [bNI])
        copy(NF, NI, [bNI], [bNF])
        W, bW = new()
        stt(W, NF, -2.0 * math.pi, ANG, ALU.mult, ALU.add, [bNF, bANG], [bW])
        ts(W, W, 0.25, None, ALU.mult, None, [bW], [bW])
        W2, bW2 = new()
        tt(W2, W, W, ALU.mult, [bW], [bW2])
        SS_, bSS_ = new()
        CC_, bCC_ = new()
        cs = [(-1.0) ** i / math.factorial(2 * i + 1) for i in range(7)]
        cc = [(-1.0) ** i / math.factorial(2 * i) for i in range(8)]
        ts(SS_, W2, cs[6], cs[5], ALU.mult, ALU.add, [bW2], [bSS_])
        for i in (4, 3, 2, 1, 0):
            tt(SS_, SS_, W2, ALU.mult, [bSS_, bW2], [bSS_])
            ts(SS_, SS_, cs[i], None, ALU.add, None, [bSS_], [bSS_])
        tt(SS_, SS_, W, ALU.mult, [bSS_, bW], [bSS_])
        ts(CC_, W2, cc[7], cc[6], ALU.mult, ALU.add, [bW2], [bCC_])
        for i in (5, 4, 3, 2, 1, 0):
            tt(CC_, CC_, W2, ALU.mult, [bCC_, bW2], [bCC_])
            ts(CC_, CC_, cc[i], None, ALU.add, None, [bCC_], [bCC_])
        T1, bT1 = new()
        for _ in range(2):
            tt(T1, SS_, SS_, ALU.mult, [bSS_], [bT1])
            tt(SS_, SS_, CC_, ALU.mult, [bSS_, bCC_], [bSS_])
            ts(SS_, SS_, 2.0, None, ALU.mult, None, [bSS_], [bSS_])
            ts(CC_, T1, -2.0, 1.0, ALU.mult, ALU.add, [bT1], [bCC_])
        PR, bPR = new((128, 11, 256))
        PI, bPI = new((128, 11, 256))
        PSG, bPSG = new((128, 11, 256))
        tt(PR[:, 0, :], MAG, CC_, ALU.mult, [bMAG, bCC_], [bPR])
        tt(PI[:, 0, :], MAG, SS_, ALU.mult, [bMAG, bSS_], [bPI])
        DEN, bDEN = new()
        tt(DEN, LR, LR, ALU.mult, [bSP3], [bDEN])
        tt(T1, LI, LI, ALU.mult, [bSP3], [bT1])
        tt(DEN, DEN, T1, ALU.add, [bDEN, bT1], [bDEN])
        recip(DEN, DEN, [bDEN], [bDEN])
        A1, bA1 = new()
        ts(A1, PR[:, 0, :], -1.0, None, ALU.add, None, [bPR], [bA1])
        FRE, bFRE = new()
        FIM, bFIM = new()
        T2, bT2 = new()
        tt(FRE, A1, LR, ALU.mult, [bA1, bSP3], [bFRE])
        tt(T2, PI[:, 0, :], LI, ALU.mult, [bPI, bSP3], [bT2])
        tt(FRE, FRE, T2, ALU.add, [bFRE, bT2], [bFRE])
        tt(FRE, FRE, DEN, ALU.mult, [bFRE, bDEN], [bFRE])
        tt(FIM, PI[:, 0, :], LR, ALU.mult, [bPI, bSP3], [bFIM])
        tt(T2, A1, LI, ALU.mult, [bA1, bSP3], [bT2])
        tt(FIM, FIM, T2, ALU.subtract, [bFIM, bT2], [bFIM])
        tt(FIM, FIM, DEN, ALU.mult, [bFIM, bDEN], [bFIM])
        CB, bCB = new()
        ts(CB, FIM, P("sgn_tb"), None, ALU.mult, None, [bFIM, b_par], [bCB])
        for k in range(1, 11):
            tt(T1, PR[:, k - 1, :], PR[:, k - 1, :], ALU.mult, [bPR], [bT1])
            tt(T2, PI[:, k - 1, :], PI[:, k - 1, :], ALU.mult, [bPI], [bT2])
            tt(PR[:, k, :], T1, T2, ALU.subtract, [bT1, bT2], [bPR])
            tt(T1, PR[:, k - 1, :], PI[:, k - 1, :], ALU.mult, [bPR, bPI], [bT1])
            ts(PI[:, k, :], T1, 2.0, None, ALU.mult, None, [bT1], [bPI])
        for k in range(11):
            ts(PSG[:, k, :], PI[:, k, :], P("nsgn"), None, ALU.mult, None, [bPI, b_par], [bPSG])
        SHB, bSHB = new((128, 128), BF16)
        copy(SHB, P("shift64"), [b_par], [bSHB])

        XNC, bXNC = new((128, NSEQ, S), BF16)
        HB = [[(new((128, S), BF16)[0], [Buf("hbt%d_%d_%d" % (i_, d, j)) for j in range(S // T)]) for d in range(2)]
              for i_ in range(2)]
        Y, bY = new((128, NSEQ, S), F32)
        GO, bGO = new((128, NSEQ, S), BF16)
        UP, bUP = new((128, 2, 3, 128), F32)
        XALL = [new((128, 128), F32) for _ in range(2)]
        WB = [[new((128, 128), BF16) for d in range(2)] for gg in range(8)]
        WC = [[new((128, 128), BF16) for d in range(2)] for gg in range(8)]
        AKS = [[[new((128, 128), BF16) for k in range(11)] for d in range(2)] for _ in range(2)]
        TG, bTG = new((128, 512), F32)
        NJ = S // T
        hset = 0
        cast_i = 0

        def cast(out_ap, in_ap, reads, writes):
            nonlocal cast_i
            if cast_i % 2 == 0:
                act(out_ap, in_ap, AF.Copy, reads, writes)
            else:
                copy(out_ap, in_ap, reads, writes)
            cast_i += 1

        def acc_mm(ps_ap, psbuf, lhsT, rhs, reads):
            sch.op("pe", lambda e: e.matmul(ps_ap, lhsT, rhs, start=False, stop=True, skip_group_check=True),
                   reads=reads + [psbuf], writes=[psbuf])

        for ch in range(16):
            dma("sp", XNC, xnT[ch * 128:(ch + 1) * 128, :].rearrange("p (q t) -> p q t", q=NSEQ), b_xnT[0:NT], [bXNC], "xnc")
            base = 768 + ch * 2 * 3 * 128
            dma("sp", UP, s5par[:, base:base + 768].rearrange("p (d w c) -> p d w c", d=2, w=3), [], [bUP], "up")
            for d in range(2):
                XA, bXA = XALL[d]
                for gg in range(8):
                    col = d * 128 + ch * 8 + gg
                    ts(XA[:, gg * 16:(gg + 1) * 16], UP[:, d, 0, gg * 16:(gg + 1) * 16], FRE[:, col:col + 1], None,
                       ALU.mult, None, [bUP, bFRE], [bXA])
                    stt(XA[:, gg * 16:(gg + 1) * 16], UP[:, d, 1, gg * 16:(gg + 1) * 16], CB[:, col:col + 1],
                        XA[:, gg * 16:(gg + 1) * 16], ALU.mult, ALU.add, [bUP, bCB, bXA], [bXA])
                ps, pb = nextps()
                sch.op("pe", lambda e, ps=ps, XA=XA: e.transpose(ps[:, 0:128], XA, IDF), reads=[bXA, b_par], writes=[pb])
                for gg in range(8):
                    wbt, bwbt = WB[gg][d]
                    ts(wbt, ps[:, 0:128], P("gmask", gg), None, ALU.mult, None, [pb, b_par], [bwbt])
                    wct, bwct = WC[gg][d]
                    memset(wct, 0.0, [bwct])
                    ts(wct[:, gg * 16:(gg + 1) * 16], UP[:, d, 2, gg * 16:(gg + 1) * 16], P("nsgn"), None, ALU.mult, None,
                       [bUP, b_par], [bwct], eng="pool")
            for gg in range(8):
                aks = AKS[gg % 2]
                for d in range(2):
                    col = d * 128 + ch * 8 + gg
                    for k in range(11):
                        ak, bak = aks[d][k]
                        act(ak, IDB, AF.Copy, [b_cst, bPR], [bak], scale=PR[:, k, col:col + 1])
                        stt(ak, SHB, PSG[:, k, col:col + 1], ak, ALU.mult, ALU.add, [bSHB, bPSG, bak], [bak])
                for q in range(NSEQ):
                    hb = HB[hset % 2]
                    hset += 1
                    for d in range(2):
                        Hb, bHb = hb[d]
                        wbt, bwbt = WB[gg][d]
                        for j in range(NJ):
                            bk = 4 * d + j
                            mm_group(psum[bk][:, :], psb[bk], [(wbt, XNC[:, q, j * 512:(j + 1) * 512])], [bwbt, bXNC])
                            cast(Hb[:, j * 512:(j + 1) * 512], psum[bk][:, :], [psb[bk]], [bHb[j]])
                    for k in range(11):
                        sft = 1 << k
                        for d in range(2):
                            Hb, bHb = hb[d]
                            ak, bak = aks[d][k]
                            work = []
                            for j in range(NJ):
                                if d == 0:
                                    d0 = max(512 * j, sft)
                                    d1 = 512 * (j + 1)
                                    s0 = d0 - sft
                                else:
                                    d0 = 512 * j
                                    d1 = min(512 * (j + 1), S - sft)
                                    s0 = d0 + sft
                                if d0 >= d1:
                                    continue
                                w = d1 - d0
                                bk = 4 * d + j
                                srcb = [bHb[t_] for t_ in range(s0 // 512, (s0 + w - 1) // 512 + 1)]
                                acc_mm(psum[bk][:, d0 - 512 * j:d0 - 512 * j + w], psb[bk], ak, Hb[:, s0:s0 + w], [bak] + srcb)
                                work.append((bk, j, d0, w))
                            for (bk, j, d0, w) in work:
                                cast(Hb[:, d0:d0 + w], psum[bk][:, d0 - 512 * j:d0 - 512 * j + w], [psb[bk]], [bHb[j]])
                    for j in range(NJ):
                        mm_group(psum[j][:, :], psb[j], [(WC[gg][0][0], hb[0][0][:, j * 512:(j + 1) * 512]),
                                                         (WC[gg][1][0], hb[1][0][:, j * 512:(j + 1) * 512])],
                                 [WC[gg][0][1], WC[gg][1][1], hb[0][1][j], hb[1][1][j]])
                        if gg == 0:
                            copy(Y[:, q, j * 512:(j + 1) * 512], psum[j][:, :], [psb[j]], [bY])
                        else:
                            tt(Y[:, q, j * 512:(j + 1) * 512], Y[:, q, j * 512:(j + 1) * 512], psum[j][:, :], ALU.add,
                               [psb[j], bY], [bY])
            for q in range(NSEQ):
                for j in range(NJ):
                    sl = slice(j * 512, (j + 1) * 512)
                    stt(Y[:, q, sl], XNC[:, q, sl], P("s5_d", ch), Y[:, q, sl], ALU.mult, ALU.add, [bXNC, bY, b_par], [bY])
                    tt(TG, Y[:, q, sl], Y[:, q, sl], ALU.mult, [bY], [bTG], eng="pool")
                    ts(TG, TG, 0.044715, 1.0, ALU.mult, ALU.add, [bTG], [bTG], eng="pool")
                    tt(TG, TG, Y[:, q, sl], ALU.mult, [bTG, bY], [bTG], eng="pool")
                    act(TG, TG, AF.Sigmoid, [bTG], [bTG], scale=2.0 * math.sqrt(2.0 / math.pi))
                    tt(GO[:, q, sl], TG, Y[:, q, sl], ALU.mult, [bTG, bY], [bGO])
            dma("pool", catT[ch * 128:(ch + 1) * 128, :].rearrange("p (q t) -> p q t", q=NSEQ), GO, [bGO],
                b_catA[0:NT], "go")
        sch.barrier(skip=("cast",))

    def phase_F():
        ar = Arena(PERSIST_END)
        XT2 = [ar.alloc([128, 16, 512], F32, "xt%d" % i) for i in range(2)]
        bXT2 = [[Buf("xt%d_%d" % (i, c)) for c in range(16)] for i in range(2)]
        XN = ar.alloc([128, 16, 512], BF16, "xn")
        bXN = [Buf("xn%d" % c) for c in range(16)]
        RSTD = ar.alloc([128, 512], F32, "rstd")
        bRSTD = Buf("rstd")
        XO2 = [ar.alloc([128, 16, 512], F32, "xo%d" % i) for i in range(2)]
        bXO2 = [[Buf("xo%d_%d" % (i, c)) for c in range(16)] for i in range(2)]
        OUT = [ar.alloc([128, 2048], F32, "out%d" % i) for i in range(2)]
        bOUT = [Buf("out%d" % i) for i in range(2)]
        oi = 0
        for n in range(NT):
            t0 = n * T
            XT, bXT, XO, bXO = XT2[n % 2], bXT2[n % 2], XO2[n % 2], bXO2[n % 2]
            dma("sp", XT, xT[:, t0:t0 + T].rearrange("(c p) t -> p c t", p=128), [b_xT[n], b_xT[min(n + 1, NT - 1)]],
                bXT, "xt_ld%d" % (n % 2))
            for c in range(16):
                tt(XN[:, c, :], XT[:, c, :], XT[:, c, :], ALU.mult, [bXT[c]], [bXN[c]], eng=("dve" if c % 2 == 0 else "pool"))
            ps, pb = nextps()
            mm_group(ps[:, :], pb, [(ONB, XN[:, c, :]) for c in range(16)], bXN + [b_cst])
            act(RSTD, ps[:, :], AF.Sqrt, [pb], [bRSTD], bias=RMS_EPS, scale=1.0 / D)
            recip(RSTD, RSTD, [bRSTD], [bRSTD])
            for c in range(16):
                stt(XO[:, c, :], XT[:, c, :], P("g_final", c), RSTD, ALU.mult, ALU.mult, [bXT[c], bRSTD, b_par], [bXO[c]])
            for s in range(4):
                O = OUT[oi % 2]
                bO = bOUT[oi % 2]
                key = "out%d" % (oi % 2)
                oi += 1
                for g in range(4):
                    ps, pb = nextps()

                    def fn(e, ps=ps, g=g, s=s, XO=XO):
                        inst = None
                        for cl in range(4):
                            c = g * 4 + cl
                            inst = e.transpose(ps[:, cl * 128:(cl + 1) * 128], XO[:, c, s * 128:(s + 1) * 128], IDF)
                        return inst
                    sch.op("pe", fn, reads=bXO[g * 4:g * 4 + 4] + [b_par], writes=[pb])
                    if g % 2 == 0:
                        act(O[:, g * 512:(g + 1) * 512], ps[:, :], AF.Copy, [pb], [bO])
                    else:
                        copy(O[:, g * 512:(g + 1) * 512], ps[:, :], [pb], [bO])
                dma("pool", out[t0 + s * 128:t0 + (s + 1) * 128, :], O, [bO], [], key)
        sch.barrier(skip=("cast",))

    A1 = Arena(PERSIST_END)
    KM = [A1.alloc([128, 16, 256], BF16, "km%d" % b) for b in range(NSEQ)]
    VM = [A1.alloc([128, 2, 2048], BF16, "vm%d" % b) for b in range(NSEQ)]
    bKM = [Buf("km%d" % b) for b in range(NSEQ)]
    bVM = [Buf("vm%d" % b) for b in range(NSEQ)]
    MEM_END = A1.off

    seq = phases["seq"]
    for ph in seq:
        if ph == "A0":
            phase_A0()
        elif ph == "B0":
            phase_B0()
        elif ph == "C0":
            phase_C0()
        elif ph == "M0":
            phase_M(0, KM, bKM, VM, bVM)
        elif ph == "D0":
            phase_D(0, "cat")
        elif ph == "E0":
            phase_E(0)
        elif ph == "N1":
            phase_N1()
        elif ph == "S5":
            phase_S5()
        elif ph == "M1":
            phase_M(1, KM, bKM, VM, bVM)
        elif ph == "D1":
            phase_D(1, "s5")
        elif ph == "E1":
            phase_E(1)
        elif ph == "F":
            phase_F()
        else:
            raise ValueError(ph)
    emit_casts(len(cast_queue))
    sch.barrier()
    sch.emit(nc, es)
    es.close()
    return nc, sch


FULL_SEQ = ["A0", "B0", "C0", "M0", "D0", "E0", "N1", "S5", "M1", "D1", "E1", "F"]


def prepare_inputs(inputs):
    c = make_consts()
    c = make_params(inputs, c)
    blob = c.build()
    wmap = {
        "ab_w_in": inputs["ab_w_in"][0], "ab_w_out": inputs["ab_w_out"][0],
        "s5_w_val": inputs["s5_w_val"][0], "s5_w_gate": inputs["s5_w_gate"][0],
    }
    for l in range(2):
        for nm in ("xa_wq", "xa_wk", "xa_wv", "xa_wo", "ffn_w_up", "ffn_w_down"):
            wmap["%s%d" % (nm, l)] = inputs[nm][l]
    wmap = {k: np.ascontiguousarray(np.asarray(v, np.float32)) for k, v in wmap.items()}
    return c.off, blob, wmap, make_s5par(inputs)


def kernel(**inputs):
    poff, blob, wmap, s5p = prepare_inputs(inputs)
    phases = {"cast": True, "seq": FULL_SEQ}
    nc, sch = build_program(poff, blob.shape[1], phases)
    x = np.asarray(inputs["x"], np.float32)
    mem = np.asarray(inputs["mem"], np.float32)
    in_maps = []
    for i in range(NCORES):
        m = {"x_in": np.ascontiguousarray(x[2 * i:2 * i + 2].reshape(TOK, D)),
             "mem_in": np.ascontiguousarray(mem[2 * i:2 * i + 2].reshape(NSEQ * MEM, D)),
             "par_in": blob, "s5par": s5p}
        m.update(wmap)
        in_maps.append(m)
    res = run_bass_kernel_spmd(nc, in_maps, core_ids=list(range(NCORES)))
    outs = [np.asarray(r["out"]).reshape(NSEQ, S, D) for r in res.results]
    return np.concatenate(outs, axis=0).astype(np.float32)
```
